# Optimizing a Trainium2 kernel written in Bass

```python
import math
import jax, jax.numpy as jnp
from jax import lax
import numpy as np


D_MODEL = 1024
BATCH = 32
SEQ = 2048
DEPTH = 1
DEC_BATCH = 4
DEC_SEQ = 8192
PAST_LEN = 128

GRID_W = 64
Q_BLOCK = 128
EPS = 1e-6
HA_Q = 8
HA_KV = 2
G_A = HA_Q // HA_KV
HD_A = 128
ROPE_AXIS_DIM = HD_A // 2
ROPE_THETA = 10000.0
HB = 8
DH_B = 64
DV_B = 2 * DH_B
N_BUCKETS = 32
MAX_DIST = 128
D_FF = 2816
CONV_W = 3
WA_Q = HA_Q * HD_A
WA_KV = HA_KV * HD_A
WB_QK = HB * 2 * DH_B
WB_V = HB * DV_B
N_IN = WA_Q + 2 * WA_KV + 2 * WB_QK + WB_V + 2 * D_MODEL

kernel_name = "hybrid_gqa_diffattn_convffn_encoder"


def rmsnorm(x, g):
    xf = x.astype(jnp.float32)
    y = xf * lax.rsqrt(jnp.mean(xf * xf, axis=-1, keepdims=True) + EPS)
    return (y * g.astype(jnp.float32)).astype(x.dtype)


def lambda_init(l):
    return 0.8 - 0.6 * math.exp(-0.3 * l)


def axial_rope_tables(n):
    rows = n // GRID_W
    row = jnp.repeat(jnp.arange(rows, dtype=jnp.float32), GRID_W)
    col = jnp.tile(jnp.arange(GRID_W, dtype=jnp.float32), rows)
    inv = ROPE_THETA ** (-jnp.arange(0, ROPE_AXIS_DIM, 2, dtype=jnp.float32) / ROPE_AXIS_DIM)
    ang_r = row[:, None] * inv[None, :]
    ang_c = col[:, None] * inv[None, :]
    return jnp.cos(ang_r), jnp.sin(ang_r), jnp.cos(ang_c), jnp.sin(ang_c)


def _rotate(x, cos, sin):
    half = x.shape[-1] // 2
    x1, x2 = x[..., :half], x[..., half:]
    cos = cos[:, None, :].astype(x.dtype)
    sin = sin[:, None, :].astype(x.dtype)
    return jnp.concatenate([x1 * cos - x2 * sin, x2 * cos + x1 * sin], axis=-1)


def apply_axial_rope(x, tabs):
    cr, sr, cc, sc = tabs
    return jnp.concatenate([_rotate(x[..., :ROPE_AXIS_DIM], cr, sr),
                            _rotate(x[..., ROPE_AXIS_DIM:], cc, sc)], axis=-1)


def t5_bucket(rel):
    nb = N_BUCKETS // 2
    ret = (rel > 0).astype(jnp.int32) * nb
    n = jnp.abs(rel)
    max_exact = nb // 2
    large = max_exact + (jnp.log(jnp.maximum(n, 1).astype(jnp.float32) / max_exact)
                         / math.log(MAX_DIST / max_exact) * (nb - max_exact)).astype(jnp.int32)
    large = jnp.minimum(large, nb - 1)
    return ret + jnp.where(n < max_exact, n, large)


def to_blocks(t):
    b, n = t.shape[:2]
    return jnp.moveaxis(t.reshape((b, n // Q_BLOCK, Q_BLOCK) + t.shape[2:]), 1, 0)


def from_blocks(t):
    t = jnp.moveaxis(t, 0, 1)
    return t.reshape((t.shape[0], t.shape[1] * t.shape[2]) + t.shape[3:])


def gqa_mixer(q, k, v):
    scale = HD_A ** -0.5

    def block(qb):
        s = jnp.einsum('bqkgd,bskd->bkgqs', qb, k).astype(jnp.float32) * scale
        p = jax.nn.softmax(s, axis=-1).astype(v.dtype)
        return jnp.einsum('bkgqs,bskd->bqkgd', p, v)

    return from_blocks(lax.map(block, to_blocks(q)))


def diff_mixer(q1, q2, k1, k2, v, lam, rel_bias):
    n = k1.shape[1]
    scale = DH_B ** -0.5
    kpos = jnp.arange(n, dtype=jnp.int32)
    qpos = kpos.reshape(n // Q_BLOCK, Q_BLOCK)

    def block(args):
        q1b, q2b, qp = args
        rel = kpos[None, :] - qp[:, None]
        bias = jnp.moveaxis(rel_bias[t5_bucket(rel)], -1, 0).astype(jnp.float32)
        s1 = jnp.einsum('bqhd,bshd->bhqs', q1b, k1).astype(jnp.float32) * scale + bias
        s2 = jnp.einsum('bqhd,bshd->bhqs', q2b, k2).astype(jnp.float32) * scale + bias
        a = (jax.nn.softmax(s1, axis=-1) - lam * jax.nn.softmax(s2, axis=-1)).astype(v.dtype)
        return jnp.einsum('bhqs,bshe->bqhe', a, v)

    return from_blocks(lax.map(block, (to_blocks(q1), to_blocks(q2), qpos)))


def depthwise_conv(g, w, b):
    n = g.shape[1]
    pad = CONV_W // 2
    gp = jnp.pad(g, ((0, 0), (pad, CONV_W - 1 - pad), (0, 0)))
    z = b
    for j in range(CONV_W):
        z = z + gp[:, j:j + n] * w[j]
    return z


def encoder_layer(x, c, l, tabs, p):
    b, n, _ = x.shape
    mod = jnp.einsum('bd,de->be', jax.nn.silu(c), p['w_mod'][l]) + p['b_mod'][l]
    sh1, sc1, gt1, sh2, sc2, gt2 = [m[:, None, :] for m in jnp.split(mod, 6, axis=-1)]

    h = rmsnorm(x, p['g_norm1'][l]) * (1 + sc1) + sh1
    proj = jnp.einsum('bnd,de->bne', h, p['w_in'][l])
    widths = [WA_Q, WA_KV, WA_KV, WB_QK, WB_QK, WB_V, D_MODEL, D_MODEL]
    idx = [int(s) for s in np.cumsum(widths)[:-1]]
    qa, ka, va, qb, kb, vb, ga, gb = jnp.split(proj, idx, axis=-1)

    qa = apply_axial_rope(rmsnorm(qa.reshape(b, n, HA_Q, HD_A), p['g_qnorm'][l]), tabs)
    ka = apply_axial_rope(rmsnorm(ka.reshape(b, n, HA_KV, HD_A), p['g_knorm'][l]), tabs)
    va = va.reshape(b, n, HA_KV, HD_A)
    o_a = gqa_mixer(qa.reshape(b, n, HA_KV, G_A, HD_A), ka, va).reshape(b, n, WA_Q)

    qb = qb.reshape(b, n, HB, 2, DH_B)
    kb = kb.reshape(b, n, HB, 2, DH_B)
    vb = vb.reshape(b, n, HB, DV_B)
    f32 = jnp.float32
    lam = (jnp.exp(jnp.sum(p['lambda_q1'][l].astype(f32) * p['lambda_k1'][l].astype(f32)))
           - jnp.exp(jnp.sum(p['lambda_q2'][l].astype(f32) * p['lambda_k2'][l].astype(f32)))
           + lambda_init(l))
    o_b = diff_mixer(qb[..., 0, :], qb[..., 1, :], kb[..., 0, :], kb[..., 1, :], vb, lam, p['rel_bias'])
    o_b = (rmsnorm(o_b, p['g_subln'][l]) * (1.0 - lambda_init(l))).reshape(b, n, WB_V)

    merged = jax.nn.sigmoid(ga) * o_a + jax.nn.sigmoid(gb) * o_b
    x = x + gt1 * jnp.einsum('bnd,de->bne', merged, p['w_out'][l])

    h2 = rmsnorm(x, p['g_norm2'][l]) * (1 + sc2) + sh2
    u, g = jnp.split(jnp.einsum('bnd,df->bnf', h2, p['w_ffn_in'][l]), 2, axis=-1)
    a = jax.nn.gelu(depthwise_conv(g, p['conv_w'][l], p['conv_b'][l]), approximate=False) * u
    x = x + gt2 * jnp.einsum('bnf,fd->bnd', a, p['w_down'][l])
    return x


def encoder_trunk(x, c, p):
    tabs = axial_rope_tables(x.shape[1])
    for l in range(DEPTH):
        x = encoder_layer(x, c, l, tabs, p)
    return rmsnorm(x, p['g_final'])


def setup_inputs(seed: int = 0) -> dict:
    key = jax.random.key(seed)
    ks = jax.random.split(key, 24)
    f32 = jnp.float32
    nrm = lambda k, shape, s: jax.random.normal(k, shape, f32) * s
    gain = lambda k, shape: 1.0 + 0.01 * jax.random.normal(k, shape, f32)
    D = D_MODEL
    return {
        'x_prompt': nrm(ks[0], (BATCH, SEQ, D), 1.0),
        'x_sample': nrm(ks[1], (DEC_BATCH, DEC_SEQ, D), 1.0),
        'c_prompt': nrm(ks[2], (BATCH, D), 1.0),
        'c_sample': nrm(ks[3], (DEC_BATCH, D), 1.0),
        'w_mod': nrm(ks[4], (DEPTH, D, 6 * D), 0.5 * D ** -0.5),
        'b_mod': nrm(ks[5], (DEPTH, 6 * D), 0.01),
        'g_norm1': gain(ks[6], (DEPTH, D)),
        'w_in': nrm(ks[7], (DEPTH, D, N_IN), D ** -0.5),
        'g_qnorm': gain(ks[8], (DEPTH, HD_A)),
        'g_knorm': gain(ks[9], (DEPTH, HD_A)),
        'lambda_q1': nrm(ks[10], (DEPTH, DH_B), 0.1),
        'lambda_k1': nrm(ks[11], (DEPTH, DH_B), 0.1),
        'lambda_q2': nrm(ks[12], (DEPTH, DH_B), 0.1),
        'lambda_k2': nrm(ks[13], (DEPTH, DH_B), 0.1),
        'g_subln': gain(ks[14], (DEPTH, DV_B)),
        'rel_bias': nrm(ks[15], (N_BUCKETS, HB), 0.5),
        'w_out': nrm(ks[16], (DEPTH, D, D), D ** -0.5),
        'g_norm2': gain(ks[17], (DEPTH, D)),
        'w_ffn_in': nrm(ks[18], (DEPTH, D, 2 * D_FF), D ** -0.5),
        'conv_w': nrm(ks[19], (DEPTH, CONV_W, D_FF), CONV_W ** -0.5),
        'conv_b': nrm(ks[20], (DEPTH, D_FF), 0.01),
        'w_down': nrm(ks[21], (DEPTH, D_FF, D), D_FF ** -0.5),
        'g_final': gain(ks[22], (D,)),
    }


def reference(x_prompt, x_sample, c_prompt, c_sample, w_mod, b_mod, g_norm1, w_in, g_qnorm, g_knorm,
              lambda_q1, lambda_k1, lambda_q2, lambda_k2, g_subln, rel_bias, w_out, g_norm2,
              w_ffn_in, conv_w, conv_b, w_down, g_final):
    params = {
        'w_mod': w_mod, 'b_mod': b_mod, 'g_norm1': g_norm1, 'w_in': w_in,
        'g_qnorm': g_qnorm, 'g_knorm': g_knorm,
        'lambda_q1': lambda_q1, 'lambda_k1': lambda_k1, 'lambda_q2': lambda_q2, 'lambda_k2': lambda_k2,
        'g_subln': g_subln, 'rel_bias': rel_bias, 'w_out': w_out, 'g_norm2': g_norm2,
        'w_ffn_in': w_ffn_in, 'conv_w': conv_w, 'conv_b': conv_b, 'w_down': w_down, 'g_final': g_final,
    }
    y_prompt = encoder_trunk(x_prompt, c_prompt, params)
    y_sample = encoder_trunk(x_sample, c_sample, params)
    return (y_prompt, y_sample)
```

```python
import math
import numpy as np
import concourse.bass as bass
import concourse.mybir as mybir
from concourse.bass_utils import run_bass_kernel_spmd
from contextlib import ExitStack

F32 = mybir.dt.float32
BF16 = mybir.dt.bfloat16
ALU = mybir.AluOpType
AF = mybir.ActivationFunctionType
AX = mybir.AxisListType

D = 1024
NIN = 6656
DFF = 2816
NFC = 22
EPS = 1e-6
C_QA, C_KA, C_VA, C_QB, C_KB, C_VB, C_GA, C_GB = 0, 1024, 1280, 1536, 2560, 3584, 4608, 5632
SW = 1152
GL = 1280
VW = 130
EPOCH = 20000
SC_A = 128.0 ** -0.5
SC_B = 64.0 ** -0.5
LAMBDA_INIT = 0.8 - 0.6 * math.exp(0.0)


class Sched:
    ENG = ("pe", "act", "dve", "pool", "sp")

    def __init__(self, nc, stack):
        self.nc = nc
        self.stack = stack
        self.ops = {e: [] for e in self.ENG}
        self.reg = {}
        self.dma_cnt = {}
        self.dma_sems = {}

    def op(self, eng, fn, reads=(), writes=(), dma=None):
        deps = {}

        def add(tok, raw):
            kind, src, val = tok
            if kind == "eng" and src == eng and eng == "pe":
                return
            k = (kind, src)
            if deps.get(k, -1) < val:
                deps[k] = val

        for r in reads:
            st = self.reg.get(r)
            if st and st["w"] is not None:
                add(st["w"], True)
            if st and isinstance(r, tuple) and r[0] == "B":
                for k, v in st["r"].items():
                    if not (k[0] == "eng" and k[1] == eng):
                        add((k[0], k[1], v), False)
        for w in writes:
            st = self.reg.get(w)
            if st:
                if st["w"] is not None:
                    add(st["w"], False)
                for k, v in st["r"].items():
                    add((k[0], k[1], v), False)
        seq = len(self.ops[eng])
        if dma is not None:
            if dma not in self.dma_sems:
                self.dma_sems[dma] = None
                self.dma_cnt[dma] = 0
            self.dma_cnt[dma] += 16
            tok = ("dma", dma, self.dma_cnt[dma])
        else:
            tok = ("eng", eng, seq)
        for r in reads:
            st = self.reg.setdefault(r, {"w": None, "r": {}})
            k = (tok[0], tok[1])
            if st["r"].get(k, -1) < tok[2]:
                st["r"][k] = tok[2]
        for w in writes:
            self.reg[w] = {"w": tok, "r": {}}
        self.ops[eng].append({"fn": fn, "deps": deps, "dma": dma, "seq": seq})
        return tok

    def emit(self):
        nc = self.nc
        need = {e: set() for e in self.ENG}
        for e in self.ENG:
            for o in self.ops[e]:
                for (kind, src), val in o["deps"].items():
                    if kind == "eng":
                        need[src].add(val)
        sigidx = {}
        esems = {}
        for e in self.ENG:
            m = {}
            for c, s in enumerate(sorted(need[e])):
                m[s] = c + 1
            sigidx[e] = m
            n_ep = len(m) // EPOCH + 1
            esems[e] = [self.stack.enter_context(nc.semaphore("e_%s_%d" % (e, i))) for i in range(n_ep)]
        for i, k in enumerate(self.dma_sems):
            self.dma_sems[k] = self.stack.enter_context(nc.semaphore("d_%d" % i))

        def sem_of(e, cnt):
            ep = (cnt - 1) // EPOCH
            return esems[e][ep], cnt - ep * EPOCH, ep

        final = [(k, self.dma_cnt[k]) for k in self.dma_sems]
        block = self.stack.enter_context(nc.Block())

        def run(e, eng):
            water = {}
            for o in self.ops[e]:
                for (kind, src), val in o["deps"].items():
                    if kind == "eng":
                        sem, v, ep = sem_of(src, sigidx[src][val])
                        k = ("eng", src, ep)
                    else:
                        sem, v = self.dma_sems[src], val
                        k = ("dma", src)
                    if water.get(k, -1) >= v:
                        continue
                    water[k] = v
                    eng.wait_ge(sem, v)
                inst = o["fn"](eng)
                if o["dma"] is not None:
                    inst.then_inc(self.dma_sems[o["dma"]], 16)
                elif o["seq"] in sigidx[e]:
                    sem, v, ep = sem_of(e, sigidx[e][o["seq"]])
                    inst.then_inc(sem, 1)
            if e == "sp":
                for k, v in final:
                    eng.wait_ge(self.dma_sems[k], v)

        @block.tensor
        def _(eng):
            run("pe", eng)

        @block.scalar
        def _(eng):
            run("act", eng)

        @block.vector
        def _(eng):
            run("dve", eng)

        @block.gpsimd
        def _(eng):
            run("pool", eng)

        @block.sync
        def _(eng):
            run("sp", eng)


def build_program(NP, n_p, n_s, n_own, KVB):
    NJ = NP + 1
    n_max = max(n_p, n_s)
    TB = KVB // 128
    nc = bass.Bass("TRN2", target_bir_lowering=False)

    def din(name, shape, dt=F32):
        return nc.dram_tensor(name, list(shape), dt, kind="ExternalInput").ap()

    def dscr(name, shape, dt):
        return nc.dram_tensor(name, list(shape), dt, kind="Internal").ap()

    xp = din("xp", [NP, n_p, D])
    xs = din("xs", [n_s, D])
    cT = din("cT", [D, NJ])
    rope_p = din("rope_p", [n_p, 256])
    rope_s = din("rope_s", [n_s, 256])
    oh = din("oh", [32, GL])
    rb_far = din("rb_far", [1, 16])
    w_mod = din("w_mod", [D, 6 * D])
    b_mod = din("b_mod", [1, 6 * D])
    g_norm1 = din("g_norm1", [1, D])
    g_norm2 = din("g_norm2", [1, D])
    g_final = din("g_final", [1, D])
    w_in = din("w_in", [D, NIN])
    g_qnorm = din("g_qnorm", [1, 128])
    g_knorm = din("g_knorm", [1, 128])
    g_subln = din("g_subln", [1, 128])
    lam_in = din("lam_in", [4, 64])
    rel_bias = din("rel_bias", [32, 8])
    w_out = din("w_out", [D, D])
    w_ffn = din("w_ffn", [D, 2 * DFF])
    conv_wb = din("conv_wb", [4, DFF])
    w_down = din("w_down", [DFF, D])
    yp = nc.dram_tensor("yp", [NP, n_p, D], F32, kind="ExternalOutput").ap()
    ys = nc.dram_tensor("ys", [n_own, D], F32, kind="ExternalOutput").ap()

    w_in_b = dscr("w_in_b", [D, NIN], BF16)
    w_out_b = dscr("w_out_b", [D, D], BF16)
    w_ffn_b = dscr("w_ffn_b", [D, 2 * DFF], BF16)
    w_down_b = dscr("w_down_b", [DFF, D], BF16)
    KTs = dscr("KTs", [10, 128, n_max], BF16)
    Vs = dscr("Vs", [10, 128, n_max // 128, VW], BF16)
    Gs = dscr("Gs", [8, GL], F32)
    stripsb = dscr("stripsb", [8, 128, SW], BF16)
    modbc = dscr("modbc", [NJ, 2 * D], F32)

    stack = ExitStack()
    with stack:
        S = Sched(nc, stack)
        sb = nc.alloc_sbuf_tensor
        xsl = [sb("xsl%d" % i, [128, 4, D], F32) for i in range(2)]
        tokbf = sb("tokbf", [128, 4, D], BF16)
        featbf = sb("featbf", [128, 8, 512], BF16)
        ring = [sb("ring%d" % i, [128, 8, 512], BF16) for i in range(4)]
        big = sb("big", [128, 32 * 512], BF16)
        kvK = [sb("kvK%d" % i, [128, KVB], BF16) for i in range(3)]
        kvV = [sb("kvV%d" % i, [128, TB, VW], BF16) for i in range(3)]
        PT = [sb("PT%d" % i, [128, 2, 512], BF16) for i in range(2)]
        strip = sb("strip", [128, GL], F32)
        rtmp = sb("rtmp", [128, 2048], F32)
        oa = sb("oa", [128, 4, 128], F32)
        wk = [sb("wk%d" % i, [128, D], F32) for i in range(4)]
        gfbc = sb("gfbc", [128, D], F32)
        gtbc = sb("gtbc", [128, 2 * D], F32)
        gq_bc = sb("gq_bc", [128, 128], F32)
        gq_sw = sb("gq_sw", [128, 128], F32)
        gk_bc = sb("gk_bc", [128, 128], F32)
        gk_sw = sb("gk_sw", [128, 128], F32)
        gsub8 = sb("gsub8", [128, 128], F32)
        ropet = sb("ropet", [128, 4, 256], F32)
        qkb = sb("qkb", [128, 4, 512], BF16)
        osb = sb("osb", [128, 8, VW], F32)
        stripb = sb("stripb", [128, SW], BF16)
        ss = sb("ss", [128, 16], F32)
        st4 = sb("st4", [128, 32], F32)
        modF = sb("modF", [128, 4, 8, NJ], F32)
        A12 = sb("A12", [128, 2, 8, NJ], F32)
        gT = sb("gT", [128, 16], F32)
        bmT = sb("bmT", [128, 48], F32)
        cwT = sb("cwT", [128, 4, NFC], F32)
        lamc = sb("lamc", [128, 4], F32)
        btab = sb("btab", [128, 24], F32)
        halo = sb("halo", [128, 2, 4, NFC], F32)
        apend = sb("apend", [128, NFC], BF16)
        pendz = sb("pendz", [128, NFC], F32)
        identb = sb("identb", [128, 128], BF16)
        identf = sb("identf", [128, 128], F32)
        Jf = sb("Jf", [128, 128], F32)
        cTs = sb("cTs", [128, 8, NJ], F32)
        scs = sb("scs", [128, 8, NJ], F32)
        g8 = sb("g8", [16, 128], F32)
        c88 = sb("c88", [88, 128], F32)
        bm48 = sb("bm48", [48, 128], F32)
        rb32 = sb("rb32", [32, 8], F32)
        pb = nc.alloc_psum_tensor("pb", [128, 8, 512], F32)

        def BK(i):
            return ("B", i)

        def pbT(i):
            return pb[:, i, :].bitcast(BF16)

        def bigk(off, n):
            return [("big", i) for i in range(off // 512, (off + n + 511) // 512)]

        def V(fn, r=(), w=()):
            return S.op("dve", fn, r, w)

        def A(fn, r=(), w=()):
            return S.op("act", fn, r, w)

        def P(fn, r=(), w=()):
            return S.op("pe", fn, r, w)

        def G(fn, r=(), w=()):
            return S.op("pool", fn, r, w)

        def LD(fn, r, w, sem):
            return S.op("sp", fn, r, w, dma=sem)

        def STO(fn, r, w, sem):
            return S.op("pool", fn, r, w, dma=sem)

        QTA = big[:, 0:4096].rearrange("p (h n) -> p h n", h=8)
        QTB = big[:, 4096:8192].rearrange("p (h n) -> p h n", h=8)
        sg = big[:, 8192:16384].rearrange("p (t c) -> p t c", t=4)
        aT = big[:, 0:NFC * 512].rearrange("p (f n) -> p f n", f=NFC)
        KTst = big[:, 0:5120].rearrange("p (h n) -> p h n", h=10)
        Vst = big[:, 5120:5120 + 10 * 4 * VW].rearrange("p (h t c) -> p h t c", h=10, t=4)
        K_KTst = bigk(0, 5120)
        K_Vst = bigk(5120, 10 * 4 * VW)

        def cast(dst, src, r0, r1, key):
            STO(lambda e: e.dma_start(out=dst[r0:r1, :], in_=src[r0:r1, :]), [], [key], ("cast", key))

        for i in range(4):
            cast(w_in_b, w_in, i * 256, (i + 1) * 256, ("w_in_b", i))
        G(lambda e: e.memset(identb[:], 0.0), [], ["identb"])
        G(lambda e: e.affine_select(out=identb[:], in_=identb[:], compare_op=ALU.not_equal, fill=1.0, base=0,
                                    pattern=[[-1, 128]], channel_multiplier=1), ["identb"], ["identb"])
        G(lambda e: e.memset(identf[:], 0.0), [], ["identf"])
        G(lambda e: e.affine_select(out=identf[:], in_=identf[:], compare_op=ALU.not_equal, fill=1.0, base=0,
                                    pattern=[[-1, 128]], channel_multiplier=1), ["identf"], ["identf"])
        G(lambda e: e.memset(Jf[:], 0.0), [], ["Jf"])
        G(lambda e: e.affine_select(out=Jf[:], in_=Jf[:], compare_op=ALU.not_equal, fill=1.0, base=-127,
                                    pattern=[[1, 128]], channel_multiplier=1), ["Jf"], ["Jf"])
        for i in range(2):
            cast(w_ffn_b, w_ffn, i * 512, (i + 1) * 512, ("w_ffn_b", i))
        cast(w_out_b, w_out, 0, D, ("w_out_b", 0))
        cast(w_down_b, w_down, 0, DFF, ("w_down_b", 0))
        K_WIN = [("w_in_b", i) for i in range(4)]
        K_WFFN = [("w_ffn_b", i) for i in range(2)]
        K_WOUT = [("w_out_b", 0)]
        K_WDN = [("w_down_b", 0)]

        LD(lambda e: e.dma_start(out=cTs[:], in_=cT.rearrange("(k p) s -> p k s", p=128)), [], ["cTs"], "m0")
        LD(lambda e: e.dma_start(out=g8[0:8, :], in_=g_norm1.rearrange("o (k p) -> (o k) p", p=128)), [], ["g8"], "m1")
        LD(lambda e: e.dma_start(out=g8[8:16, :], in_=g_norm2.rearrange("o (k p) -> (o k) p", p=128)), [], ["g8"], "m1")
        LD(lambda e: e.dma_start(out=c88[:], in_=conv_wb.rearrange("j (c p) -> (j c) p", p=128)), [], ["c88"], "m2")
        LD(lambda e: e.dma_start(out=bm48[:], in_=b_mod.rearrange("o (c p) -> (o c) p", p=128)), [], ["bm48"], "m3")
        LD(lambda e: e.dma_start(out=rb32[:], in_=rel_bias), [], ["rb32"], "m4")
        LD(lambda e: e.dma_start(out=rtmp[0:32, 0:GL], in_=oh), [], ["rtmp"], "m5")
        LD(lambda e: e.dma_start(out=btab[:, 0:16], in_=rb_far.partition_broadcast(128)), [], ["btab"], "m6")
        V(lambda e: e.tensor_tensor(out=btab[:, 16:24], in0=btab[:, 8:16], in1=btab[:, 0:8], op=ALU.subtract), ["btab"], ["btab"])
        LD(lambda e: e.dma_start(out=gfbc[:], in_=g_final.partition_broadcast(128)), [], ["gfbc"], "m7")
        LD(lambda e: e.dma_start(out=gq_bc[:], in_=g_qnorm.partition_broadcast(128)), [], ["gq_bc"], "m8")
        LD(lambda e: e.dma_start(out=gk_bc[:], in_=g_knorm.partition_broadcast(128)), [], ["gk_bc"], "m9")
        LD(lambda e: e.dma_start(out=gsub8[:], in_=g_subln.partition_broadcast(128)), [], ["gsub8"], "m10")
        for (dst, src, key, sem) in ((gq_sw, g_qnorm, "gq_sw", "m11"), (gk_sw, g_knorm, "gk_sw", "m12")):
            for a in range(4):
                so = (a ^ 1) * 32
                LD(lambda e, dst=dst, src=src, a=a, so=so: e.dma_start(
                    out=dst[:, a * 32:(a + 1) * 32], in_=src[:, so:so + 32].partition_broadcast(128)), [], [key], sem)
        LD(lambda e: e.dma_start(out=wk[0][:, 0:256], in_=lam_in.rearrange("a (o d) -> o (a d)", o=1).partition_broadcast(128)),
           [], ["wk0"], "m13")

        V(lambda e: e.tensor_scalar(out=gsub8[:], in0=gsub8[:], scalar1=1.0 - LAMBDA_INIT, scalar2=0.0, op0=ALU.mult, op1=ALU.add),
          ["gsub8"], ["gsub8"])
        V(lambda e: e.tensor_tensor(out=wk[0][:, 256:320], in0=wk[0][:, 0:64], in1=wk[0][:, 64:128], op=ALU.mult), ["wk0"], ["wk0"])
        V(lambda e: e.tensor_tensor(out=wk[0][:, 320:384], in0=wk[0][:, 128:192], in1=wk[0][:, 192:256], op=ALU.mult), ["wk0"], ["wk0"])
        V(lambda e: e.tensor_reduce(out=lamc[:, 0:2], in_=wk[0][:, 256:384].rearrange("p (a d) -> p a d", a=2), axis=AX.X, op=ALU.add),
          ["wk0"], ["lamc"])
        A(lambda e: e.activation(out=lamc[:, 0:2], in_=lamc[:, 0:2], func=AF.Exp), ["lamc"], ["lamc"])
        V(lambda e: e.tensor_tensor(out=lamc[:, 2:3], in0=lamc[:, 1:2], in1=lamc[:, 0:1], op=ALU.subtract), ["lamc"], ["lamc"])
        V(lambda e: e.tensor_scalar(out=lamc[:, 3:4], in0=lamc[:, 2:3], scalar1=-LAMBDA_INIT, scalar2=0.0, op0=ALU.add, op1=ALU.add),
          ["lamc"], ["lamc"])
        NEGLAM = lamc[:, 3:4]

        P(lambda e: e.matmul(pb[:, 0, 0:16], lhsT=g8[0:16, :], rhs=identf[0:16, 0:16], start=True, stop=True), ["g8", "identf"], [BK(0)])
        V(lambda e: e.tensor_copy(out=gT[:], in_=pb[:, 0, 0:16]), [BK(0)], ["gT"])
        P(lambda e: e.matmul(pb[:, 1, 0:88], lhsT=c88[0:88, :], rhs=identf[0:88, 0:88], start=True, stop=True), ["c88", "identf"], [BK(1)])
        V(lambda e: e.tensor_copy(out=cwT[:].rearrange("p j c -> p (j c)"), in_=pb[:, 1, 0:88]), [BK(1)], ["cwT"])
        P(lambda e: e.matmul(pb[:, 2, 0:48], lhsT=bm48[0:48, :], rhs=identf[0:48, 0:48], start=True, stop=True), ["bm48", "identf"], [BK(2)])
        V(lambda e: e.tensor_copy(out=bmT[:], in_=pb[:, 2, 0:48]), [BK(2)], ["bmT"])

        A(lambda e: e.activation(out=scs[:], in_=cTs[:], func=AF.Silu), ["cTs"], ["scs"])
        LD(lambda e: e.dma_start(out=wk[1][0:NJ, :], in_=b_mod[:, 2 * D:3 * D].partition_broadcast(NJ)), [], ["wk1"], "m14")
        LD(lambda e: e.dma_start(out=wk[2][0:NJ, :], in_=b_mod[:, 5 * D:6 * D].partition_broadcast(NJ)), [], ["wk2"], "m15")
        kinds = {0: 0, 1: 1, 3: 2, 4: 3}
        li = 0
        for blk in range(6):
            for hf in range(2):
                slot = li % 2
                li += 1
                wm = xsl[slot][:].rearrange("p t d -> p (t d)").rearrange("p (k c) -> p k c", k=8)
                c0 = blk * D + hf * 512
                LD(lambda e, wm=wm, c0=c0: e.dma_start(out=wm, in_=w_mod[:, c0:c0 + 512].rearrange("(k p) c -> p k c", p=128)),
                   [], [("x", slot)], ("x", slot))
                if blk in kinds:
                    ki = kinds[blk]
                    for j in range(4):
                        col = (ki * 8 + hf * 4 + j) * NJ
                        for k in range(8):
                            P(lambda e, wm=wm, j=j, k=k, col=col: e.matmul(
                                pb[:, 6, col:col + NJ], lhsT=wm[:, k, j * 128:(j + 1) * 128], rhs=scs[:, k, :],
                                start=(k == 0), stop=(k == 7)), [("x", slot), "scs"], [BK(6)])
                else:
                    gi = 0 if blk == 2 else 1
                    for k in range(8):
                        P(lambda e, wm=wm, k=k: e.matmul(pb[0:NJ, 7, :], lhsT=scs[:, k, :], rhs=wm[:, k, :],
                                                         start=(k == 0), stop=(k == 7)), [("x", slot), "scs"], [BK(7)])
                    bsrc = wk[1] if gi == 0 else wk[2]
                    V(lambda e, gi=gi, hf=hf, bsrc=bsrc: e.tensor_tensor(
                        out=wk[3][0:NJ, hf * 512:(hf + 1) * 512], in0=pb[0:NJ, 7, :], in1=bsrc[0:NJ, hf * 512:(hf + 1) * 512], op=ALU.add),
                      [BK(7), "wk1", "wk2"], ["wk3"])
                    if hf == 1:
                        STO(lambda e, gi=gi: e.dma_start(out=modbc[:, gi * D:(gi + 1) * D], in_=wk[3][0:NJ, :]), ["wk3"],
                            ["modbc", "pst_serial"], "pst")
        for ki in range(4):
            blk = [0, 1, 3, 4][ki]
            V(lambda e, ki=ki, blk=blk: e.tensor_tensor(
                out=modF[:, ki, :, :], in0=pb[:, 6, ki * 8 * NJ:(ki + 1) * 8 * NJ].rearrange("p (a b) -> p a b", a=8),
                in1=bmT[:, blk * 8:(blk + 1) * 8].unsqueeze(2).broadcast_to([128, 8, NJ]), op=ALU.add), [BK(6), "bmT"], ["modF"])
        for a, ki in ((0, 1), (1, 3)):
            V(lambda e, a=a, ki=ki: e.tensor_scalar(out=A12[:, a, :, :], in0=modF[:, ki, :, :], scalar1=1.0, scalar2=0.0, op0=ALU.add, op1=ALU.add),
              ["modF"], ["A12"])
            V(lambda e, a=a: e.tensor_tensor(out=A12[:, a, :, :], in0=A12[:, a, :, :],
                                             in1=gT[:, a * 8:(a + 1) * 8].unsqueeze(2).broadcast_to([128, 8, NJ]), op=ALU.mult),
              ["A12", "gT"], ["A12"])

        for ci, (c0, cn) in enumerate(((0, 512), (512, 512), (1024, 256))):
            P(lambda e, c0=c0, cn=cn, ci=ci: e.matmul(pb[0:8, 3 + ci, 0:cn], lhsT=rb32[0:32, 0:8], rhs=rtmp[0:32, c0:c0 + cn],
                                                       start=True, stop=True), ["rb32", "rtmp"], [BK(3 + ci)])
            V(lambda e, c0=c0, cn=cn, ci=ci: e.tensor_copy(out=strip[0:8, c0:c0 + cn], in_=pb[0:8, 3 + ci, 0:cn]), [BK(3 + ci)], ["strip"])
        STO(lambda e: e.dma_start(out=Gs, in_=strip[0:8, :]), ["strip"], ["Gs", "pst_serial"], "pst")
        for h in range(8):
            hank = bass.AP(tensor=Gs.tensor, offset=h * GL, ap=[[1, 128], [1, SW]])
            LD(lambda e, hank=hank: e.dma_start(out=rtmp[:, 0:SW], in_=hank), ["Gs"], ["rtmp"], "m5")
            for ci, (c0, cn) in enumerate(((0, 512), (512, 512), (1024, 128))):
                bk = 3 + ci
                P(lambda e, c0=c0, cn=cn, bk=bk: e.matmul(pb[:, bk, 0:cn], lhsT=Jf[:], rhs=rtmp[:, c0:c0 + cn], start=True, stop=True),
                  ["Jf", "rtmp"], [BK(bk)])
                V(lambda e, c0=c0, cn=cn, bk=bk: e.tensor_copy(out=strip[:, c0:c0 + cn], in_=pb[:, bk, 0:cn]), [BK(bk)], ["strip"])
            V(lambda e, h=h: e.tensor_scalar(out=stripb[:], in0=strip[:, 0:SW], scalar1=btab[:, h:h + 1], scalar2=1.0 / SC_B,
                                             op0=ALU.subtract, op1=ALU.mult), ["strip", "btab"], ["stripb"])
            STO(lambda e, h=h: e.dma_start(out=stripsb[h], in_=stripb[:]), ["stripb"], ["stripsb", "pst_serial"], "pst")

        ring_i = [0]

        def wtile(Wb, wkeys, r0, nk, c0, ncol):
            slot = ring_i[0] % 4
            ring_i[0] += 1
            LD(lambda e: e.dma_start(out=ring[slot][:, 0:nk, 0:ncol],
                                     in_=Wb[r0:r0 + nk * 128, c0:c0 + ncol].rearrange("(k p) c -> p k c", p=128)),
               wkeys, [("ring", slot)], ("ring", slot))
            return ring[slot], ("ring", slot)

        bank_i = [0]

        def nbank(lst):
            b = lst[bank_i[0] % len(lst)]
            bank_i[0] += 1
            return b

        cp_i = [0]

        def evac(out_ap, in_ap, r, w):
            cp_i[0] += 1
            if cp_i[0] % 2:
                A(lambda e: e.activation(out=out_ap, in_=in_ap, func=AF.Copy), r, w)
            else:
                V(lambda e: e.tensor_copy(out=out_ap, in_=in_ap), r, w)

        def rstd_from(ssap, n, r, w, outap):
            V(lambda e: e.tensor_scalar(out=outap, in0=ssap, scalar1=1.0 / n, scalar2=EPS, op0=ALU.mult, op1=ALU.add), r, w)
            A(lambda e: e.activation(out=outap, in_=outap, func=AF.Sqrt), w, w)
            V(lambda e: e.reciprocal(out=outap, in_=outap), w, w)

        def norm_p1(xkey, xt, nt, stat, c0, k0, k1):
            for t in range(nt):
                A(lambda e, t=t: e.activation(out=wk[3][:], in_=xt[:, t, :], func=AF.Square, accum_out=stat[:, c0 + t:c0 + t + 1]),
                  [xkey], ["wk3", k0])
            rstd_from(stat[:, c0:c0 + nt], float(D), [k0], [k1], stat[:, c0 + 8:c0 + 8 + nt])
            for t in range(nt):
                V(lambda e, t=t: e.tensor_scalar(out=tokbf[:, t, :], in0=xt[:, t, :], scalar1=stat[:, c0 + 8 + t:c0 + 9 + t], scalar2=0.0,
                                                 op0=ALU.mult, op1=ALU.add), [xkey, k1], [("tok", t)])

        def norm_p2(a_idx, sh_ki, jidx, nt, ks, banks):
            N = nt * 128
            for k in ks:
                bk = banks[k % len(banks)]
                for t in range(nt):
                    P(lambda e, t=t, k=k, bk=bk: e.transpose(out=pbT(bk)[:, t * 128:(t + 1) * 128], in_=tokbf[:, t, k * 128:(k + 1) * 128],
                                                              identity=identb[:]), [("tok", t), "identb"], [BK(bk)])
                V(lambda e, k=k, bk=bk: e.tensor_scalar(out=featbf[:, k, 0:N], in0=pbT(bk)[:, 0:N], scalar1=A12[:, a_idx, k, jidx:jidx + 1],
                                                        scalar2=modF[:, sh_ki, k, jidx:jidx + 1], op0=ALU.mult, op1=ALU.add),
                  [BK(bk), "A12", "modF"], [("feat", k)])

        def norm_feat(xkey, xt, a_idx, sh_ki, jidx, nt):
            norm_p1(xkey, xt, nt, ss, 0, "ss", "ss8")
            norm_p2(a_idx, sh_ki, jidx, nt, range(8), [4, 5, 6, 7])

        FEAT = [("feat", k) for k in range(8)]
        TOK = [("tok", t) for t in range(4)]

        def rope_apply(src, srckeys, H, Cg, Sg, g_keys, outbf, outkeys, wa, wb, wc, wak, wbk, wck):
            n = H * 128
            A(lambda e: e.activation(out=wa[:, 0:n], in_=src, func=AF.Copy), srckeys, [wak])
            V(lambda e: e.tensor_tensor(out=wb[:, 0:n], in0=wa[:, 0:n], in1=wa[:, 0:n], op=ALU.mult), [wak], [wbk])
            V(lambda e: e.tensor_reduce(out=st4[:, 0:H], in_=wb[:, 0:n].rearrange("p (h d) -> p h d", h=H), axis=AX.X, op=ALU.add),
              [wbk], ["st4"])
            rstd_from(st4[:, 0:H], 128.0, ["st4"], ["st4b"], st4[:, 8:8 + H])
            V(lambda e: e.tensor_tensor(out=wb[:, 0:n].rearrange("p (h d) -> p h d", h=H), in0=wa[:, 0:n].rearrange("p (h d) -> p h d", h=H),
                                        in1=Cg.unsqueeze(1).broadcast_to([128, H, 128]), op=ALU.mult), [wak] + g_keys, [wbk])
            for bsel in range(2):
                for ax in range(2):
                    xin = wa[:, 0:n].rearrange("p (h r) -> p h r", h=H)[:, :, ax * 64 + (1 - bsel) * 32: ax * 64 + (1 - bsel) * 32 + 32]
                    xout = wc[:, 0:n].rearrange("p (h r) -> p h r", h=H)[:, :, ax * 64 + bsel * 32: ax * 64 + bsel * 32 + 32]
                    sgin = Sg[:, ax * 64 + bsel * 32: ax * 64 + bsel * 32 + 32].unsqueeze(1).broadcast_to([128, H, 32])
                    V(lambda e, xin=xin, xout=xout, sgin=sgin: e.tensor_tensor(out=xout, in0=xin, in1=sgin, op=ALU.mult),
                      [wak] + g_keys, [wck])
            V(lambda e: e.tensor_tensor(out=wb[:, 0:n], in0=wb[:, 0:n], in1=wc[:, 0:n], op=ALU.add), [wbk, wck], [wbk])
            V(lambda e: e.tensor_tensor(out=outbf.rearrange("p (h d) -> p h d", h=H), in0=wb[:, 0:n].rearrange("p (h d) -> p h d", h=H),
                                        in1=st4[:, 8:8 + H].unsqueeze(2).broadcast_to([128, H, 128]), op=ALU.mult),
              [wbk, "st4b"], outkeys)

        def rope_tables(gbc, gsw, gkeys, nt):
            V(lambda e: e.tensor_tensor(out=ropet[:, 0:nt, 0:128], in0=ropet[:, 0:nt, 0:128],
                                        in1=gbc[:].unsqueeze(1).broadcast_to([128, nt, 128]), op=ALU.mult), ["ropet"] + gkeys, ["ropet"])
            V(lambda e: e.tensor_tensor(out=ropet[:, 0:nt, 128:256], in0=ropet[:, 0:nt, 128:256],
                                        in1=gsw[:].unsqueeze(1).broadcast_to([128, nt, 128]), op=ALU.mult), ["ropet"] + gkeys, ["ropet"])

        jobs = []
        for j in range(NP):
            jobs.append(dict(kind="p", j=j, n_tot=n_p, n_q=n_p, halo=False, x=xp[j], rope=rope_p, y=yp[j]))
        jobs.append(dict(kind="s", j=NP, n_tot=n_s, n_q=n_own, halo=True, x=xs, rope=rope_s, y=ys))
        units = []
        for job in jobs:
            for ck in range(job["n_tot"] // 512):
                units.append(dict(kind="kv", job=job, tok0=ck * 512, nt=4, ck=ck))
            nqc = job["n_q"] // 512
            for c in range(nqc):
                units.append(dict(kind="q", job=job, tok0=c * 512, nt=4, c=c, first=(c == 0), last=(c == nqc - 1 and not job["halo"])))
            if job["halo"]:
                units.append(dict(kind="h", job=job, tok0=job["n_q"], nt=1, c=nqc, first=False, last=False))
        for i, u in enumerate(units):
            u["slot"] = i % 2
            u["next"] = units[i + 1] if i + 1 < len(units) else None

        def load_x(u):
            job, tok0, nt, slot = u["job"], u["tok0"], u["nt"], u["slot"]
            LD(lambda e: e.dma_start(out=xsl[slot][:, 0:nt, :], in_=job["x"][tok0:tok0 + nt * 128, :].rearrange("(t p) d -> p t d", p=128)),
               [], [("x", slot)], ("x", slot))

        def load_rope(u):
            job, tok0, nt = u["job"], u["tok0"], u["nt"]
            LD(lambda e: e.dma_start(out=ropet[:, 0:nt, :], in_=job["rope"][tok0:tok0 + nt * 128, :].rearrange("(t p) d -> p t d", p=128)),
               [], ["ropet"], "ropet")

        kv_i = [0]
        ALLB = [0, 1, 2, 3, 4, 5]
        ALL8 = [0, 1, 2, 3, 4, 5, 6, 7]

        def unit_kv(u):
            job, tok0, slot, ck = u["job"], u["tok0"], u["slot"], u["ck"]
            jidx = job["j"]
            xk = ("x", slot)
            xt = xsl[slot]
            load_rope(u)
            if not u.get("prenormed"):
                norm_feat(xk, xt, 0, 0, jidx, 4)
            rope_tables(gk_bc, gk_sw, ["gk_bc", "gk_sw"], 4)
            V(lambda e: e.memset(Vst[:, :, :, 128:VW], 1.0), [], K_Vst)
            W, wkey = wtile(w_in_b, K_WIN, 0, 8, C_KA, 512)
            for t in range(4):
                bk = nbank(ALLB)
                for k in range(8):
                    P(lambda e, t=t, k=k, bk=bk, W=W: e.matmul(pb[:, bk, :], lhsT=featbf[:, k, t * 128:(t + 1) * 128], rhs=W[:, k, :],
                                                               start=(k == 0), stop=(k == 7)), [("feat", k), wkey], [BK(bk)])
                A(lambda e, t=t, bk=bk: e.activation(out=Vst[:, 0:2, t, 0:128], in_=pb[:, bk, 256:512].rearrange("p (h d) -> p h d", h=2),
                                                     func=AF.Copy), [BK(bk)], K_Vst)
                rope_apply(pb[:, bk, 0:256], [BK(bk)], 2, ropet[:, t, 0:128], ropet[:, t, 128:256], ["ropet"],
                           qkb[:, t, 0:256], [("qkb", t)], wk[0], wk[1], wk[2], "wk0", "wk1", "wk2")

            def ka_transposes():
                for t in range(4):
                    for h2 in range(2):
                        P(lambda e, t=t, h2=h2: e.transpose(out=pbT(7)[:, h2 * 512 + t * 128: h2 * 512 + (t + 1) * 128],
                                                             in_=qkb[:, t, h2 * 128:(h2 + 1) * 128], identity=identb[:]),
                          [("qkb", t), "identb"], [BK(7)])
                evac(KTst[:, 0:2, :], pbT(7)[:, 0:1024].rearrange("p (h n) -> p h n", h=2), [BK(7)], K_KTst)
            if u.get("next") is not None:
                load_x(u["next"])
            for half in range(2):
                W, wkey = wtile(w_in_b, K_WIN, 0, 8, C_KB + half * 512, 512)
                for hh in range(4):
                    h = half * 4 + hh
                    bk = nbank(ALLB)
                    for k in range(8):
                        P(lambda e, hh=hh, k=k, bk=bk, W=W: e.matmul(pb[:, bk, :], lhsT=W[:, k, hh * 128:(hh + 1) * 128], rhs=featbf[:, k, :],
                                                                     start=(k == 0), stop=(k == 7)), [("feat", k), wkey], [BK(bk)])
                    evac(KTst[:, 2 + h, :], pb[:, bk, :], [BK(bk)], K_KTst)
                if half == 0:
                    ka_transposes()
            un = u.get("next")
            if un is not None:
                nslot = un["slot"]
                norm_p1(("x", nslot), xsl[nslot], un["nt"], st4, 4, "st4p0", "st4p1")
                un["prenormed"] = True
            for half in range(2):
                W, wkey = wtile(w_in_b, K_WIN, 0, 8, C_VB + half * 512, 512)
                for t in range(4):
                    bk = nbank(ALLB)
                    for k in range(8):
                        P(lambda e, t=t, k=k, bk=bk, W=W: e.matmul(pb[:, bk, :], lhsT=featbf[:, k, t * 128:(t + 1) * 128], rhs=W[:, k, :],
                                                                   start=(k == 0), stop=(k == 7)), [("feat", k), wkey], [BK(bk)])
                    evac(Vst[:, 2 + half * 4:6 + half * 4, t, 0:128], pb[:, bk, :].rearrange("p (h d) -> p h d", h=4), [BK(bk)], K_Vst)
            if un is not None:
                norm_p2(0, 0, un["job"]["j"], un["nt"], range(8), [4, 5, 6, 7])
            T0 = tok0 // 128
            STO(lambda e: e.dma_start(out=KTs[:, :, tok0:tok0 + 512].rearrange("h p n -> p h n"), in_=KTst), K_KTst,
                [("kvs", ck), "kvs_serial"], "kvst")
            STO(lambda e: e.dma_start(out=Vs[:, :, T0:T0 + 4, :].rearrange("h p t c -> p h (t c)"),
                                      in_=Vst.rearrange("p h t c -> p h (t c)")), K_Vst, [("kvs", ck), "kvs_serial"], "kvst")

        OPOS = [(4, 0), (4, VW), (4, 2 * VW), (5, 0), (5, VW), (5, 2 * VW), (6, 0), (6, VW)]

        def attention(u, NQ, nt):
            job = u["job"]
            q0 = u["tok0"]
            nblk = job["n_tot"] // KVB
            entries = []
            for h in range(8):
                for isB in (False, True):
                    hd = dict(isB=isB, h=h, touched=set(), scale=SC_B if isB else SC_A)
                    if not isB:
                        hd["Oacc"] = [[OPOS[j] for j in range(nt)]]
                    else:
                        hd["Oacc"] = [[OPOS[j] for j in range(nt)], [OPOS[4 + j] for j in range(nt)]]
                    for b in range(nblk):
                        entries.append(dict(hd=hd, hh=(2 + h) if isB else (h // 4), b=b, slot=kv_i[0] % 3))
                        kv_i[0] += 1

            def rec_load(k):
                if k >= len(entries):
                    return
                ent = entries[k]
                slot, b, hh = ent["slot"], ent["b"], ent["hh"]
                cks = [("kvs", c_) for c_ in range(b * KVB // 512, (b + 1) * KVB // 512)]
                LD(lambda e: e.dma_start(out=kvK[slot][:], in_=KTs[hh, :, b * KVB:(b + 1) * KVB]), cks, [("kvK", slot)], ("kvK", slot))
                LD(lambda e: e.dma_start(out=kvV[slot][:].rearrange("p t c -> p (t c)"),
                                         in_=Vs[hh, :, b * TB:(b + 1) * TB, :].rearrange("p t c -> p (t c)")),
                   cks, [("kvV", slot)], ("kvV", slot))

            items = []
            for k, ent in enumerate(entries):
                hd, slot, b = ent["hd"], ent["slot"], ent["b"]
                step = 1 if hd["isB"] else 2
                for tl in range(0, TB, step):
                    items.append(dict(hd=hd, slot=slot, tls=list(range(tl, tl + step)), s0=b * KVB + tl * 128,
                                      kfirst=(k if tl == 0 else None), hfirst=(b == 0 and tl == 0),
                                      hlast=(b == nblk - 1 and tl + step >= TB)))
            rec_load(0)

            def rec_qk(i):
                it = items[i]
                hd, slot, tls = it["hd"], it["slot"], it["tls"]
                h = hd["h"]
                if it["kfirst"] is not None:
                    rec_load(it["kfirst"] + 1)
                if it["hfirst"] and hd["isB"]:
                    LD(lambda e: e.dma_start(out=stripb[:], in_=stripsb[h]), ["stripsb"], ["stripb"], "stripb")
                sb0 = (i % 2) * 2
                if hd["isB"]:
                    tl = tls[0]
                    delta = it["s0"] - q0
                    band = (delta - (NQ - 1) < 91) and (delta + 127 > -91)
                    it["band"] = band
                    P(lambda e: e.matmul(pb[:, sb0, 0:NQ], lhsT=kvK[slot][0:64, tl * 128:(tl + 1) * 128], rhs=QTB[0:64, h, 0:NQ],
                                         start=True, stop=not band), [("kvK", slot)] + bigk(4096 + h * 512, 512), [BK(sb0)])
                    P(lambda e: e.matmul(pb[:, sb0 + 1, 0:NQ], lhsT=kvK[slot][64:128, tl * 128:(tl + 1) * 128], rhs=QTB[64:128, h, 0:NQ],
                                         start=True, stop=not band, tile_position=(64, 0)), [("kvK", slot)] + bigk(4096 + h * 512, 512), [BK(sb0 + 1)])
                    if band:
                        off = 512 - delta
                        for ub in range(2):
                            P(lambda e, ub=ub: e.matmul(pb[:, sb0 + ub, 0:NQ], lhsT=identb[:], rhs=stripb[:, off:off + NQ],
                                                        start=False, stop=True), ["identb", "stripb"], [BK(sb0 + ub)])
                else:
                    for ui, tl in enumerate(tls):
                        P(lambda e, ui=ui, tl=tl: e.matmul(pb[:, sb0 + ui, 0:NQ], lhsT=kvK[slot][:, tl * 128:(tl + 1) * 128], rhs=QTA[:, h, 0:NQ],
                                                           start=True, stop=True), [("kvK", slot)] + bigk(h * 512, 512), [BK(sb0 + ui)])

            def rec_exp(i):
                it = items[i]
                hd, s0 = it["hd"], it["s0"]
                h, scale = hd["h"], hd["scale"]
                sb0 = (i % 2) * 2
                pt = PT[i % 2]
                ptk = "pt%d" % (i % 2)
                src = pb[:, sb0:sb0 + 2, 0:NQ]
                if hd["isB"]:
                    delta = s0 - q0
                    if it["band"]:
                        A(lambda e: e.activation(out=pt[:, :, 0:NQ], in_=src, func=AF.Exp, scale=scale), [BK(sb0), BK(sb0 + 1)], [ptk])
                    elif delta < 0:
                        A(lambda e: e.activation(out=pt[:, :, 0:NQ], in_=src, func=AF.Exp, scale=scale), [BK(sb0), BK(sb0 + 1)], [ptk])
                    else:
                        bcol = 16 + h
                        A(lambda e: e.activation(out=pt[:, :, 0:NQ], in_=src, func=AF.Exp, bias=btab[:, bcol:bcol + 1], scale=scale),
                          [BK(sb0), BK(sb0 + 1), "btab"], [ptk])
                else:
                    A(lambda e: e.activation(out=pt[:, :, 0:NQ], in_=src, func=AF.Exp, scale=scale), [BK(sb0), BK(sb0 + 1)], [ptk])

            def rec_pv(i):
                it = items[i]
                hd, slot, tls = it["hd"], it["slot"], it["tls"]
                isB = hd["isB"]
                pt = PT[i % 2]
                ptk = "pt%d" % (i % 2)
                for uu in range(2):
                    tl = tls[0] if isB else tls[uu]
                    acc = hd["Oacc"][uu] if isB else hd["Oacc"][0]
                    for j in range(nt):
                        bk, col = acc[j]
                        first = bk not in hd["touched"]
                        hd["touched"].add(bk)
                        P(lambda e, uu=uu, tl=tl, j=j, bk=bk, col=col, first=first: e.matmul(
                            pb[:, bk, col:col + VW], lhsT=pt[:, uu, j * 128:(j + 1) * 128], rhs=kvV[slot][:, tl, :],
                            start=first, stop=False, skip_group_check=True), [ptk, ("kvV", slot)], [BK(bk)])

            pend_ep = []

            def tick_ep(force=False):
                while pend_ep and (force or pend_ep[0][0] <= 0):
                    pend_ep.pop(0)[1]()
                for pe_ in pend_ep:
                    pe_[0] -= 1

            def epilogue(hd):
                tick_ep(force=True)
                isB, h = hd["isB"], hd["h"]
                banks = sorted(hd["touched"])
                nin = {4: 3, 5: 3, 6: 2}
                for bk in banks:
                    na = nin[bk] if (isB or bk == 4) else 1
                    i0_ = (bk - 4) * 3
                    V(lambda e, bk=bk, na=na, i0_=i0_: e.tensor_copy(out=osb[:, i0_:i0_ + na, :].rearrange("p a c -> p (a c)"),
                                                                       in_=pb[:, bk, 0:na * VW]), [BK(bk)], [("osb", bk)])
                OK = [("osb", bk) for bk in banks]
                rb = lambda c0: st4[:, c0:c0 + nt].unsqueeze(2).broadcast_to([128, nt, 128])
                if not isB:
                    V(lambda e: e.reciprocal(out=st4[:, 16:16 + nt], in_=osb[:, 0:nt, 128]), OK, ["st4c"])
                    V(lambda e: e.tensor_tensor(out=oa[:, 0:nt, :], in0=osb[:, 0:nt, 0:128], in1=rb(16), op=ALU.mult), OK + ["st4c"], ["oa"])
                    V(lambda e: e.tensor_tensor(out=oa[:, 0:nt, :], in0=oa[:, 0:nt, :], in1=sg[:, 0:nt, h * 128:(h + 1) * 128], op=ALU.mult),
                      ["oa"] + bigk(8192, 8192), ["oa"])
                    return
                t1 = wk[2][:, 0:512].rearrange("p (j d) -> p j d", j=4)
                ob = wk[3][:, 0:512].rearrange("p (j d) -> p j d", j=4)
                sq = wk[2][:, 512:1024].rearrange("p (j d) -> p j d", j=4)
                V(lambda e: e.reciprocal(out=st4[:, 16:16 + nt], in_=osb[:, 0:nt, 128]), OK, ["st4c"])
                V(lambda e: e.reciprocal(out=st4[:, 20:20 + nt], in_=osb[:, 4:4 + nt, 128]), OK, ["st4e"])
                V(lambda e: e.tensor_scalar(out=st4[:, 20:20 + nt], in0=st4[:, 20:20 + nt], scalar1=NEGLAM, scalar2=0.0, op0=ALU.mult, op1=ALU.add),
                  ["st4e", "lamc"], ["st4e"])
                V(lambda e: e.tensor_tensor(out=t1[:, 0:nt, :], in0=osb[:, 0:nt, 0:128], in1=rb(16), op=ALU.mult), OK + ["st4c"], ["wk2"])
                V(lambda e: e.tensor_tensor(out=ob[:, 0:nt, :], in0=osb[:, 4:4 + nt, 0:128], in1=rb(20), op=ALU.mult), OK + ["st4e"], ["wk3"])
                V(lambda e: e.tensor_tensor(out=ob[:, 0:nt, :], in0=ob[:, 0:nt, :], in1=t1[:, 0:nt, :], op=ALU.add), ["wk3", "wk2"], ["wk3"])
                V(lambda e: e.tensor_tensor(out=sq[:, 0:nt, :], in0=ob[:, 0:nt, :], in1=ob[:, 0:nt, :], op=ALU.mult), ["wk3"], ["wk2"])
                V(lambda e: e.tensor_reduce(out=st4[:, 24:24 + nt], in_=sq[:, 0:nt, :], axis=AX.X, op=ALU.add), ["wk2"], ["st4d"])
                V(lambda e: e.tensor_scalar(out=st4[:, 24:24 + nt], in0=st4[:, 24:24 + nt], scalar1=1.0 / 128, scalar2=EPS, op0=ALU.mult, op1=ALU.add),
                  ["st4d"], ["st4d"])

                def stage23():
                    A(lambda e: e.activation(out=st4[:, 24:24 + nt], in_=st4[:, 24:24 + nt], func=AF.Ln), ["st4d"], ["st4d"])
                    A(lambda e: e.activation(out=st4[:, 24:24 + nt], in_=st4[:, 24:24 + nt], func=AF.Exp, scale=-0.5), ["st4d"], ["st4d"])
                    V(lambda e: e.tensor_tensor(out=ob[:, 0:nt, :], in0=ob[:, 0:nt, :], in1=rb(24), op=ALU.mult), ["wk3", "st4d"], ["wk3"])
                    V(lambda e: e.tensor_tensor(out=ob[:, 0:nt, :], in0=ob[:, 0:nt, :], in1=sg[:, 0:nt, D + h * 128:D + (h + 1) * 128], op=ALU.mult),
                      ["wk3"] + bigk(8192, 8192), ["wk3"])
                    V(lambda e: e.tensor_tensor(out=tokbf[:, 0:nt, h * 128:(h + 1) * 128], in0=ob[:, 0:nt, :], in1=oa[:, 0:nt, :], op=ALU.add),
                      ["wk3", "oa"], TOK)
                pend_ep.append([5, stage23])

            n_it = len(items)
            for i in range(n_it + 1):
                if i < n_it:
                    rec_qk(i)
                    rec_exp(i)
                if i >= 1:
                    rec_pv(i - 1)
                    if items[i - 1]["hlast"]:
                        epilogue(items[i - 1]["hd"])
                tick_ep()
            tick_ep(force=True)


        def unit_q(u):
            job, tok0, slot, nt, c = u["job"], u["tok0"], u["slot"], u["nt"], u["c"]
            jidx = job["j"]
            par = c % 2
            NQ = nt * 128
            xk = ("x", slot)
            xt = xsl[slot]
            is_halo = (u["kind"] == "h")
            load_rope(u)
            if c == 0:
                LD(lambda e: e.dma_start(out=gtbc[:], in_=modbc[jidx:jidx + 1, :].partition_broadcast(128)), ["modbc"], ["gtbc"], "gtbc")
            if not u.get("prenormed"):
                norm_feat(xk, xt, 0, 0, jidx, nt)
            rope_tables(gq_bc, gq_sw, ["gq_bc", "gq_sw"], nt)
            ropeq = []
            gi_ = 0
            for half in range(2):
                W, wkey = wtile(w_in_b, K_WIN, 0, 8, C_QA + half * 512, 512)
                for t in range(nt):
                    bk = nbank(ALLB)
                    for k in range(8):
                        P(lambda e, t=t, k=k, bk=bk, W=W: e.matmul(pb[:, bk, :], lhsT=featbf[:, k, t * 128:(t + 1) * 128], rhs=W[:, k, :],
                                                                   start=(k == 0), stop=(k == 7)), [("feat", k), wkey], [BK(bk)])
                    qi = gi_
                    gi_ += 1
                    qbuf = tokbf[:, qi // 2, (qi % 2) * 512:(qi % 2) * 512 + 512]
                    qkey = ("tok", qi // 2)
                    wa_i = 0 if qi % 2 == 0 else 3
                    rope_apply(pb[:, bk, :], [BK(bk)], 4, ropet[:, t, 0:128], ropet[:, t, 128:256], ["ropet"],
                               qbuf, [qkey], wk[wa_i], wk[1], wk[2], "wk%d" % wa_i, "wk1", "wk2")

                    def do_tr(t=t, half=half, qbuf=qbuf, qkey=qkey):
                        for hq in range(4):
                            tb = 6 + hq // 2
                            P(lambda e, t=t, hq=hq, tb=tb, qbuf=qbuf: e.transpose(
                                out=pbT(tb)[:, (hq % 2) * 512 + t * 128:(hq % 2) * 512 + (t + 1) * 128],
                                in_=qbuf[:, hq * 128:(hq + 1) * 128], identity=identb[:]), [qkey, "identb"], [BK(tb)])
                        if t == nt - 1:
                            for pr in range(2):
                                h0 = half * 4 + pr * 2
                                evac(QTA[:, h0:h0 + 2, 0:NQ], pbT(6 + pr)[:, 0:1024].rearrange("p (h n) -> p h n", h=2)[:, :, 0:NQ], [BK(6 + pr)],
                                     bigk(h0 * 512, 1024))
                    ropeq.append(do_tr)
            for _ in range(min(2, len(ropeq))):
                ropeq.pop(0)()
            mcount = [0]

            def after_group():
                if ropeq and mcount[0] % 3 == 0:
                    ropeq.pop(0)()
                mcount[0] += 1

            for gi in range(4):
                W, wkey = wtile(w_in_b, K_WIN, 0, 8, C_GA + gi * 512, 512)
                for t in range(nt):
                    bk = nbank(ALLB)
                    for k in range(8):
                        P(lambda e, t=t, k=k, bk=bk, W=W: e.matmul(pb[:, bk, :], lhsT=featbf[:, k, t * 128:(t + 1) * 128], rhs=W[:, k, :],
                                                                   start=(k == 0), stop=(k == 7)), [("feat", k), wkey], [BK(bk)])
                    A(lambda e, t=t, bk=bk, gi=gi: e.activation(out=sg[:, t, gi * 512:(gi + 1) * 512], in_=pb[:, bk, :], func=AF.Sigmoid),
                      [BK(bk)], bigk(8192 + t * 2048 + gi * 512, 512))
                    after_group()
            for t in range(nt):
                V(lambda e, t=t: e.tensor_tensor(out=sg[:, t, D:2 * D].rearrange("p (h d) -> p h d", h=8),
                                                 in0=sg[:, t, D:2 * D].rearrange("p (h d) -> p h d", h=8),
                                                 in1=gsub8[:].unsqueeze(1).broadcast_to([128, 8, 128]), op=ALU.mult),
                  bigk(8192 + t * 2048 + D, D) + ["gsub8"], bigk(8192 + t * 2048 + D, D))
            for half in range(2):
                W, wkey = wtile(w_in_b, K_WIN, 0, 8, C_QB + half * 512, 512)
                for hq in range(4):
                    h = half * 4 + hq
                    bk = nbank(ALLB)
                    for k in range(8):
                        P(lambda e, hq=hq, k=k, bk=bk, W=W: e.matmul(pb[:, bk, 0:NQ], lhsT=W[:, k, hq * 128:(hq + 1) * 128], rhs=featbf[:, k, 0:NQ],
                                                                     start=(k == 0), stop=(k == 7)), [("feat", k), wkey], [BK(bk)])
                    evac(QTB[:, h, 0:NQ], pb[:, bk, 0:NQ], [BK(bk)], bigk(4096 + h * 512, 512))
                    after_group()
            while ropeq:
                ropeq.pop(0)()
            if u.get("next") is not None:
                load_x(u["next"])
            attention(u, NQ, nt)
            for k in range(8):
                bk = 4 + (k % 4)
                for t in range(nt):
                    P(lambda e, t=t, k=k, bk=bk: e.transpose(out=pbT(bk)[:, t * 128:(t + 1) * 128], in_=tokbf[:, t, k * 128:(k + 1) * 128],
                                                              identity=identb[:]), [("tok", t), "identb"], [BK(bk)])
                evac(featbf[:, k, 0:NQ], pbT(bk)[:, 0:NQ], [BK(bk)], [("feat", k)])
            for half in range(2):
                W, wkey = wtile(w_out_b, K_WOUT, 0, 8, half * 512, 512)
                for t in range(nt):
                    bk = nbank(ALLB)
                    for k in range(8):
                        P(lambda e, t=t, k=k, bk=bk, W=W: e.matmul(pb[:, bk, :], lhsT=featbf[:, k, t * 128:(t + 1) * 128], rhs=W[:, k, :],
                                                                   start=(k == 0), stop=(k == 7)), [("feat", k), wkey], [BK(bk)])
                    wt = wk[t % 2]
                    wtk = "wk%d" % (t % 2)
                    V(lambda e, bk=bk, half=half, wt=wt: e.tensor_tensor(out=wt[:, 0:512], in0=pb[:, bk, :], in1=gtbc[:, half * 512:(half + 1) * 512],
                                                                         op=ALU.mult), [BK(bk), "gtbc"], [wtk])
                    V(lambda e, t=t, half=half, wt=wt: e.tensor_tensor(out=xt[:, t, half * 512:(half + 1) * 512],
                                                                       in0=xt[:, t, half * 512:(half + 1) * 512], in1=wt[:, 0:512], op=ALU.add),
                      [xk, wtk], [xk])
            if not is_halo:
                STO(lambda e: e.dma_start(out=rtmp[0:1, par * D:(par + 1) * D], in_=xt[127:128, 3, :]), [xk], [("pendrow", par), "rtmp"], ("pendrow", par))
            norm_feat(xk, xt, 1, 2, jidx, nt)
            hprev = halo[:, 1 - par]
            hcur = halo[:, par]
            HP = ("halo", 1 - par)
            HC = ("halo", par)
            W = None
            for fc in range(NFC):
                if fc % 4 == 0:
                    ncol = min(512, DFF - fc * 128)
                    W, wkey = wtile(w_ffn_b, K_WFFN, 0, 8, DFF + fc * 128, ncol)
                bk = nbank(ALL8)
                fo = (fc % 4) * 128
                for k in range(8):
                    P(lambda e, k=k, bk=bk, W=W, fo=fo: e.matmul(pb[:, bk, 0:NQ], lhsT=W[:, k, fo:fo + 128], rhs=featbf[:, k, 0:NQ],
                                                                 start=(k == 0), stop=(k == 7)), [("feat", k), wkey], [BK(bk)])
                Gp = pb[:, bk, :]
                A(lambda e, fc=fc, Gp=Gp: e.activation(out=hcur[:, 3, fc:fc + 1], in_=Gp[:, 0:1], func=AF.Copy), [BK(bk)], [HC])
                if is_halo:
                    continue
                acc = wk[fc % 2]
                ak = "wk%d" % (fc % 2)
                A(lambda e, fc=fc, Gp=Gp, acc=acc: e.activation(out=acc[:, 0:NQ], in_=Gp[:, 0:NQ], func=AF.Identity,
                                                                bias=cwT[:, 3, fc:fc + 1], scale=cwT[:, 1, fc:fc + 1]), [BK(bk), "cwT"], [ak])
                V(lambda e, fc=fc, Gp=Gp, acc=acc: e.scalar_tensor_tensor(out=acc[:, 1:NQ], in0=Gp[:, 0:NQ - 1], scalar=cwT[:, 0, fc:fc + 1],
                                                                        in1=acc[:, 1:NQ], op0=ALU.mult, op1=ALU.add), [BK(bk), "cwT", ak], [ak])
                V(lambda e, fc=fc, Gp=Gp, acc=acc: e.scalar_tensor_tensor(out=acc[:, 0:NQ - 1], in0=Gp[:, 1:NQ], scalar=cwT[:, 2, fc:fc + 1],
                                                                        in1=acc[:, 0:NQ - 1], op0=ALU.mult, op1=ALU.add), [BK(bk), "cwT", ak], [ak])
                if not u["first"]:
                    V(lambda e, fc=fc, acc=acc: e.scalar_tensor_tensor(out=acc[:, 0:1], in0=hprev[:, 0, fc:fc + 1], scalar=cwT[:, 0, fc:fc + 1],
                                                                       in1=acc[:, 0:1], op0=ALU.mult, op1=ALU.add), [HP, "cwT", ak], [ak])
                A(lambda e, fc=fc, Gp=Gp: e.activation(out=hcur[:, 0, fc:fc + 1], in_=Gp[:, NQ - 1:NQ], func=AF.Copy), [BK(bk)], [HC])
                A(lambda e, fc=fc, acc=acc: e.activation(out=hcur[:, 1, fc:fc + 1], in_=acc[:, NQ - 1:NQ], func=AF.Copy), [ak], [HC])
                A(lambda e, fc=fc, acc=acc: e.activation(out=aT[:, fc, 0:NQ], in_=acc[:, 0:NQ], func=AF.Gelu), [ak], bigk(fc * 512, 512))
            if not is_halo:
                for fc in range(NFC):
                    if fc % 4 == 0:
                        ncol = min(512, DFF - fc * 128)
                        W, wkey = wtile(w_ffn_b, K_WFFN, 0, 8, fc * 128, ncol)
                    bk = nbank(ALL8)
                    fo = (fc % 4) * 128
                    for k in range(8):
                        P(lambda e, k=k, bk=bk, W=W, fo=fo: e.matmul(pb[:, bk, 0:NQ], lhsT=W[:, k, fo:fo + 128], rhs=featbf[:, k, 0:NQ],
                                                                     start=(k == 0), stop=(k == 7)), [("feat", k), wkey], [BK(bk)])
                    A(lambda e, fc=fc, bk=bk: e.activation(out=hcur[:, 2, fc:fc + 1], in_=pb[:, bk, NQ - 1:NQ], func=AF.Copy), [BK(bk)], [HC])
                    V(lambda e, fc=fc, bk=bk: e.tensor_tensor(out=aT[:, fc, 0:NQ], in0=pb[:, bk, 0:NQ], in1=aT[:, fc, 0:NQ], op=ALU.mult),
                      [BK(bk)] + bigk(fc * 512, 512), bigk(fc * 512, 512))
            have_pend = not u["first"]
            if have_pend:
                pend_prepare(hprev, HP, hcur[:, 3, :], [HC])
            hook = None
            un = u.get("next")
            if un is not None and (have_pend or not is_halo):
                nslot = un["slot"]
                norm_p1(("x", nslot), xsl[nslot], un["nt"], st4, 4, "st4p0", "st4p1")
                un["prenormed"] = True
                ksets = [[0, 1, 2], [3, 4, 5], [6, 7]]

                def hook(ti, un=un):
                    if ti < 3:
                        norm_p2(0, 0, un["job"]["j"], un["nt"], ksets[ti], [5, 6, 7])
            down_proj(u, xk, xt, 1 - par, have_pend, do_main=not is_halo, hook=hook)
            if u["last"]:
                pend_prepare(hcur, HC, None, [])
                down_proj(dict(u, tok0=tok0 + 512), xk, xt, par, True, do_main=False)

        def pend_prepare(hp, hpk, gfirst, gkeys):
            if gfirst is not None:
                V(lambda e: e.tensor_tensor(out=pendz[:], in0=cwT[:, 2, :], in1=gfirst, op=ALU.mult), ["cwT"] + gkeys, ["pendz"])
                V(lambda e: e.tensor_tensor(out=pendz[:], in0=pendz[:], in1=hp[:, 1, :], op=ALU.add), ["pendz", hpk], ["pendz"])
            else:
                V(lambda e: e.tensor_copy(out=pendz[:], in_=hp[:, 1, :]), [hpk], ["pendz"])
            A(lambda e: e.activation(out=pendz[:], in_=pendz[:], func=AF.Gelu), ["pendz"], ["pendz"])
            V(lambda e: e.tensor_tensor(out=apend[:], in0=pendz[:], in1=hp[:, 2, :], op=ALU.mult), ["pendz", hpk], ["apend"])

        def down_proj(u, xk, xt, ppar, have_pend, do_main, hook=None):
            job, tok0, nt, slot = u["job"], u["tok0"], u["nt"], u["slot"]
            NQ = nt * 128
            groups = [(0, 8), (8, 8), (16, 6)]
            for half in range(2):
                banks = [0, 1, 2, 3] if half == 0 else [5, 6, 7, 0]
                pbk = 4 if half == 0 else 1
                for gi, (f0, nf) in enumerate(groups):
                    W, wkey = wtile(w_down_b, K_WDN, f0 * 128, nf, half * 512, 512)
                    for fl in range(nf):
                        fc = f0 + fl
                        st = (fc == 0)
                        sp_ = (fc == NFC - 1)
                        if do_main:
                            for t in range(nt):
                                P(lambda e, t=t, fc=fc, fl=fl, W=W, st=st, sp_=sp_, bk=banks[t]: e.matmul(
                                    pb[:, bk, :], lhsT=aT[:, fc, t * 128:(t + 1) * 128], rhs=W[:, fl, :], start=st, stop=sp_),
                                  bigk(fc * 512, 512) + [wkey], [BK(banks[t])])
                        if have_pend:
                            P(lambda e, fc=fc, fl=fl, W=W, st=st, sp_=sp_, pbk=pbk: e.matmul(
                                pb[0:1, pbk, :], lhsT=apend[:, fc:fc + 1], rhs=W[:, fl, :], start=st, stop=sp_), ["apend", wkey], [BK(pbk)])
                    if hook is not None and half == 0:
                        hook(gi)
                if do_main:
                    for t in range(nt):
                        bk = banks[t]
                        wt = wk[t % 2]
                        wtk = "wk%d" % (t % 2)
                        V(lambda e, bk=bk, half=half, wt=wt: e.tensor_tensor(out=wt[:, 0:512], in0=pb[:, bk, :],
                                                                             in1=gtbc[:, D + half * 512:D + (half + 1) * 512], op=ALU.mult),
                          [BK(bk), "gtbc"], [wtk])
                        V(lambda e, t=t, half=half, wt=wt: e.tensor_tensor(out=xt[:, t, half * 512:(half + 1) * 512],
                                                                           in0=xt[:, t, half * 512:(half + 1) * 512], in1=wt[:, 0:512], op=ALU.add),
                          [xk, wtk], [xk])
                if have_pend:
                    V(lambda e, half=half, pbk=pbk: e.tensor_tensor(out=wk[3][0:1, half * 512:(half + 1) * 512], in0=pb[0:1, pbk, :],
                                                           in1=gtbc[0:1, D + half * 512:D + (half + 1) * 512], op=ALU.mult),
                      [BK(pbk), "gtbc"], ["wk3"])
            if have_pend:
                V(lambda e: e.tensor_tensor(out=wk[3][0:1, :], in0=wk[3][0:1, :], in1=rtmp[0:1, ppar * D:(ppar + 1) * D], op=ALU.add),
                  ["wk3", ("pendrow", ppar)], ["wk3"])
                A(lambda e: e.activation(out=wk[2][0:1, :], in_=wk[3][0:1, :], func=AF.Square, accum_out=ss[0:1, 12:13]), ["wk3"], ["wk2", "ss12"])
                rstd_from(ss[0:1, 12:13], float(D), ["ss12"], ["ss13"], ss[0:1, 13:14])
                V(lambda e: e.scalar_tensor_tensor(out=wk[3][0:1, :], in0=wk[3][0:1, :], scalar=ss[0:1, 13:14], in1=gfbc[0:1, :],
                                                   op0=ALU.mult, op1=ALU.mult), ["wk3", "ss13", "gfbc"], ["wk3"])
                STO(lambda e: e.dma_start(out=job["y"][tok0 - 1:tok0, :], in_=wk[3][0:1, :]), ["wk3"],
                    [("y", job["j"], (tok0 - 1) // 512)], "ypend")
            if do_main:
                for t in range(nt):
                    A(lambda e, t=t: e.activation(out=wk[3][:], in_=xt[:, t, :], func=AF.Square, accum_out=ss[:, t:t + 1]), [xk], ["wk3", "ss"])
                rstd_from(ss[:, 0:nt], float(D), ["ss"], ["ss8"], ss[:, 8:8 + nt])
                for t in range(nt):
                    V(lambda e, t=t: e.scalar_tensor_tensor(out=xt[:, t, :], in0=xt[:, t, :], scalar=ss[:, 8 + t:9 + t], in1=gfbc[:],
                                                            op0=ALU.mult, op1=ALU.mult), [xk, "ss8", "gfbc"], [xk])
                STO(lambda e: e.dma_start(out=job["y"][tok0:tok0 + NQ, :].rearrange("(t p) d -> p t d", p=128), in_=xt[:, 0:nt, :]),
                    [xk], [("y", job["j"], tok0 // 512)], ("yst", slot))

        if units:
            load_x(units[0])
        for i, u in enumerate(units):
            if u["kind"] == "kv":
                unit_kv(u)
            else:
                unit_q(u)
        S.emit()
    return nc


def _t5_bucket_np(rel):
    nb = 16
    ret = (rel > 0).astype(np.int32) * nb
    n = np.abs(rel)
    max_exact = nb // 2
    nf = np.maximum(n, 1).astype(np.float32)
    large = max_exact + (np.log(nf / np.float32(max_exact)) / np.float32(math.log(128 / max_exact))
                         * np.float32(nb - max_exact)).astype(np.int32)
    large = np.minimum(large, nb - 1)
    return ret + np.where(n < max_exact, n, large)


def _rope_table(pos):
    pos = np.asarray(pos)
    row = (pos // 64).astype(np.float32)
    col = (pos % 64).astype(np.float32)
    inv = (np.float32(10000.0) ** (-np.arange(0, 64, 2, dtype=np.float32) / np.float32(64))).astype(np.float32)
    ar = row[:, None] * inv[None, :]
    ac = col[:, None] * inv[None, :]
    cr, sr, cc, sc = np.cos(ar), np.sin(ar), np.cos(ac), np.sin(ac)
    C = np.concatenate([cr, cr, cc, cc], axis=1)
    Sn = np.concatenate([-sr, sr, -sc, sc], axis=1)
    return np.ascontiguousarray(np.concatenate([C, Sn], axis=1).astype(np.float32))


def _core_inputs(rev, xp_list, xsamp, own_lo, own_hi, c_list, W):
    n_p = xp_list[0].shape[0]
    n_s = xsamp.shape[0]
    if not rev:
        pos_p = np.arange(n_p)
        pos_s = np.arange(n_s)
    else:
        pos_p = np.arange(n_p)[::-1]
        pos_s = np.arange(n_s)[::-1]
    m = {}
    m["xp"] = np.ascontiguousarray(np.stack([x[pos_p] for x in xp_list]))
    m["xs"] = np.ascontiguousarray(xsamp[pos_s])
    m["cT"] = np.ascontiguousarray(np.stack(c_list, axis=1))
    m["rope_p"] = _rope_table(pos_p)
    m["rope_s"] = _rope_table(pos_s)
    mm = np.arange(GL)
    rel = 639 - mm
    if rev:
        rel = -rel
    bk = _t5_bucket_np(rel)
    ohm = np.zeros((32, GL), np.float32)
    ohm[bk, mm] = 1.0
    m["oh"] = ohm
    rb = W["rel_bias"]
    if not rev:
        m["rb_far"] = np.ascontiguousarray(np.concatenate([rb[15], rb[31]])[None, :])
    else:
        m["rb_far"] = np.ascontiguousarray(np.concatenate([rb[31], rb[15]])[None, :])
    cw = W["conv_w"][0]
    if rev:
        cw = cw[::-1]
    m["conv_wb"] = np.ascontiguousarray(np.concatenate([cw, W["conv_b"]], axis=0))
    m["w_mod"] = W["w_mod"][0]
    m["b_mod"] = W["b_mod"]
    m["g_norm1"] = W["g_norm1"]
    m["g_norm2"] = W["g_norm2"]
    m["g_final"] = W["g_final"][None, :]
    m["w_in"] = W["w_in"][0]
    m["g_qnorm"] = W["g_qnorm"]
    m["g_knorm"] = W["g_knorm"]
    m["g_subln"] = W["g_subln"]
    m["lam_in"] = np.ascontiguousarray(np.concatenate([W["lambda_q1"], W["lambda_k1"], W["lambda_q2"], W["lambda_k2"]], axis=0))
    m["rel_bias"] = W["rel_bias"]
    m["w_out"] = W["w_out"][0]
    m["w_ffn"] = W["w_ffn_in"][0]
    m["w_down"] = W["w_down"][0]
    return {k: np.ascontiguousarray(np.asarray(v, dtype=np.float32)) for k, v in m.items()}


def _own_rows(rev, n_s, own_lo, own_hi):
    if not rev:
        assert own_lo == 0
    else:
        assert own_hi == n_s


_PROG_CACHE = {}


def _get_prog(key):
    if key not in _PROG_CACHE:
        _PROG_CACHE[key] = build_program(*key)
    return _PROG_CACHE[key]


def kernel(**inputs):
    W = {k: np.asarray(v, dtype=np.float32) for k, v in inputs.items()}
    x_prompt, x_sample = W["x_prompt"], W["x_sample"]
    c_prompt, c_sample = W["c_prompt"], W["c_sample"]
    n_cores = 8
    B, n_p, _ = x_prompt.shape
    Bs, n_s, _ = x_sample.shape
    NP = B // n_cores
    n_own = n_s // 2
    nc = _get_prog((NP, n_p, n_s, n_own, 2048))
    in_maps = []
    for i in range(n_cores):
        rev = (i % 2 == 1)
        si = i // 2
        xp_list = [x_prompt[NP * i + j] for j in range(NP)]
        c_list = [c_prompt[NP * i + j] for j in range(NP)] + [c_sample[si]]
        lo, hi = (0, n_own) if not rev else (n_s - n_own, n_s)
        in_maps.append(_core_inputs(rev, xp_list, x_sample[si], lo, hi, c_list, W))
    res = run_bass_kernel_spmd(nc, in_maps, core_ids=list(range(n_cores)))
    y_prompt = np.empty((B, n_p, D), np.float32)
    y_sample = np.empty((Bs, n_s, D), np.float32)
    for i in range(n_cores):
        rev = (i % 2 == 1)
        si = i // 2
        r = res.results[i]
        yp_ = np.asarray(r["yp"], dtype=np.float32)
        ys_ = np.asarray(r["ys"], dtype=np.float32)
        if rev:
            yp_ = yp_[:, ::-1]
            y_sample[si, n_s - n_own:] = ys_[::-1]
        else:
            y_sample[si, :n_own] = ys_
        y_prompt[NP * i:NP * (i + 1)] = yp_
    return (y_prompt, y_sample)
```

```python
import math
import numpy as np
import concourse.bass as bass
import concourse.mybir as mybir
from concourse.bass_utils import run_bass_kernel_spmd
from contextlib import ExitStack

F32 = mybir.dt.float32
BF16 = mybir.dt.bfloat16
ALU = mybir.AluOpType
AF = mybir.ActivationFunctionType
AX = mybir.AxisListType

D = 1024
NIN = 6656
DFF = 2816
NFC = 22
EPS = 1e-6
C_QA, C_KA, C_VA, C_QB, C_KB, C_VB, C_GA, C_GB = 0, 1024, 1280, 1536, 2560, 3584, 4608, 5632
SW = 1152
GL = 1280
VW = 130
EPOCH = 20000
SC_A = 128.0 ** -0.5
SC_B = 64.0 ** -0.5
LAMBDA_INIT = 0.8 - 0.6 * math.exp(0.0)


class Sched:
    ENG = ("pe", "act", "dve", "pool", "sp")

    def __init__(self, nc, stack):
        self.nc = nc
        self.stack = stack
        self.ops = {e: [] for e in self.ENG}
        self.reg = {}
        self.dma_cnt = {}
        self.dma_sems = {}

    def op(self, eng, fn, reads=(), writes=(), dma=None):
        deps = {}

        def add(tok, raw):
            kind, src, val = tok
            if kind == "eng" and src == eng and eng == "pe":
                return
            k = (kind, src)
            if deps.get(k, -1) < val:
                deps[k] = val

        for r in reads:
            st = self.reg.get(r)
            if st and st["w"] is not None:
                add(st["w"], True)
            if st and isinstance(r, tuple) and r[0] == "B":
                for k, v in st["r"].items():
                    if not (k[0] == "eng" and k[1] == eng):
                        add((k[0], k[1], v), False)
        for w in writes:
            st = self.reg.get(w)
            if st:
                if st["w"] is not None:
                    add(st["w"], False)
                for k, v in st["r"].items():
                    add((k[0], k[1], v), False)
        seq = len(self.ops[eng])
        if dma is not None:
            if dma not in self.dma_sems:
                self.dma_sems[dma] = None
                self.dma_cnt[dma] = 0
            self.dma_cnt[dma] += 16
            tok = ("dma", dma, self.dma_cnt[dma])
        else:
            tok = ("eng", eng, seq)
        for r in reads:
            st = self.reg.setdefault(r, {"w": None, "r": {}})
            k = (tok[0], tok[1])
            if st["r"].get(k, -1) < tok[2]:
                st["r"][k] = tok[2]
        for w in writes:
            self.reg[w] = {"w": tok, "r": {}}
        self.ops[eng].append({"fn": fn, "deps": deps, "dma": dma, "seq": seq})
        return tok

    def emit(self):
        nc = self.nc
        need = {e: set() for e in self.ENG}
        for e in self.ENG:
            for o in self.ops[e]:
                for (kind, src), val in o["deps"].items():
                    if kind == "eng":
                        need[src].add(val)
        sigidx = {}
        esems = {}
        for e in self.ENG:
            m = {}
            for c, s in enumerate(sorted(need[e])):
                m[s] = c + 1
            sigidx[e] = m
            n_ep = len(m) // EPOCH + 1
            esems[e] = [self.stack.enter_context(nc.semaphore("e_%s_%d" % (e, i))) for i in range(n_ep)]
        for i, k in enumerate(self.dma_sems):
            self.dma_sems[k] = self.stack.enter_context(nc.semaphore("d_%d" % i))

        def sem_of(e, cnt):
            ep = (cnt - 1) // EPOCH
            return esems[e][ep], cnt - ep * EPOCH, ep

        final = [(k, self.dma_cnt[k]) for k in self.dma_sems]
        block = self.stack.enter_context(nc.Block())

        def run(e, eng):
            water = {}
            for o in self.ops[e]:
                for (kind, src), val in o["deps"].items():
                    if kind == "eng":
                        sem, v, ep = sem_of(src, sigidx[src][val])
                        k = ("eng", src, ep)
                    else:
                        sem, v = self.dma_sems[src], val
                        k = ("dma", src)
                    if water.get(k, -1) >= v:
                        continue
                    water[k] = v
                    eng.wait_ge(sem, v)
                inst = o["fn"](eng)
                if o["dma"] is not None:
                    inst.then_inc(self.dma_sems[o["dma"]], 16)
                elif o["seq"] in sigidx[e]:
                    sem, v, ep = sem_of(e, sigidx[e][o["seq"]])
                    inst.then_inc(sem, 1)
            if e == "sp":
                for k, v in final:
                    eng.wait_ge(self.dma_sems[k], v)

        @block.tensor
        def _(eng):
            run("pe", eng)

        @block.scalar
        def _(eng):
            run("act", eng)

        @block.vector
        def _(eng):
            run("dve", eng)

        @block.gpsimd
        def _(eng):
            run("pool", eng)

        @block.sync
        def _(eng):
            run("sp", eng)


def build_program(NP, n_p, n_s, n_own, KVB):
    NJ = NP + 1
    n_max = max(n_p, n_s)
    TB = KVB // 128
    nc = bass.Bass("TRN2", target_bir_lowering=False)

    def din(name, shape, dt=F32):
        return nc.dram_tensor(name, list(shape), dt, kind="ExternalInput").ap()

    def dscr(name, shape, dt):
        return nc.dram_tensor(name, list(shape), dt, kind="Internal").ap()

    xp = din("xp", [NP, n_p, D])
    xs = din("xs", [n_s, D])
    cT = din("cT", [D, NJ])
    rope_p = din("rope_p", [n_p, 256])
    rope_s = din("rope_s", [n_s, 256])
    oh = din("oh", [32, GL])
    rb_far = din("rb_far", [1, 16])
    w_mod = din("w_mod", [D, 6 * D])
    b_mod = din("b_mod", [1, 6 * D])
    g_norm1 = din("g_norm1", [1, D])
    g_norm2 = din("g_norm2", [1, D])
    g_final = din("g_final", [1, D])
    w_in = din("w_in", [D, NIN])
    g_qnorm = din("g_qnorm", [1, 128])
    g_knorm = din("g_knorm", [1, 128])
    g_subln = din("g_subln", [1, 128])
    lam_in = din("lam_in", [4, 64])
    rel_bias = din("rel_bias", [32, 8])
    w_out = din("w_out", [D, D])
    w_ffn = din("w_ffn", [D, 2 * DFF])
    conv_wb = din("conv_wb", [4, DFF])
    w_down = din("w_down", [DFF, D])
    yp = nc.dram_tensor("yp", [NP, n_p, D], F32, kind="ExternalOutput").ap()
    ys = nc.dram_tensor("ys", [n_own, D], F32, kind="ExternalOutput").ap()

    w_in_b = dscr("w_in_b", [D, NIN], BF16)
    w_out_b = dscr("w_out_b", [D, D], BF16)
    w_ffn_b = dscr("w_ffn_b", [D, 2 * DFF], BF16)
    w_down_b = dscr("w_down_b", [DFF, D], BF16)
    KTs = dscr("KTs", [10, 128, n_max], BF16)
    Vs = dscr("Vs", [10, 128, n_max // 128, VW], BF16)
    Gs = dscr("Gs", [8, GL], F32)
    stripsb = dscr("stripsb", [8, 128, SW], BF16)
    modbc = dscr("modbc", [NJ, 2 * D], F32)

    stack = ExitStack()
    with stack:
        S = Sched(nc, stack)
        sb = nc.alloc_sbuf_tensor
        xsl = [sb("xsl%d" % i, [128, 4, D], F32) for i in range(2)]
        tokbf = sb("tokbf", [128, 4, D], BF16)
        featbf = sb("featbf", [128, 8, 512], BF16)
        ring = [sb("ring%d" % i, [128, 8, 512], BF16) for i in range(4)]
        big = sb("big", [128, 32 * 512], BF16)
        kvK = [sb("kvK%d" % i, [128, KVB], BF16) for i in range(3)]
        kvV = [sb("kvV%d" % i, [128, TB, VW], BF16) for i in range(3)]
        PT = [sb("PT%d" % i, [128, 2, 512], BF16) for i in range(2)]
        strip = sb("strip", [128, GL], F32)
        rtmp = sb("rtmp", [128, 2048], F32)
        oa = sb("oa", [128, 4, 128], F32)
        wk = [sb("wk%d" % i, [128, D], F32) for i in range(4)]
        gfbc = sb("gfbc", [128, D], F32)
        gtbc = sb("gtbc", [128, 2 * D], F32)
        gq_bc = sb("gq_bc", [128, 128], F32)
        gq_sw = sb("gq_sw", [128, 128], F32)
        gk_bc = sb("gk_bc", [128, 128], F32)
        gk_sw = sb("gk_sw", [128, 128], F32)
        gsub8 = sb("gsub8", [128, 128], F32)
        ropet = sb("ropet", [128, 4, 256], F32)
        qkb = sb("qkb", [128, 4, 512], BF16)
        osb = sb("osb", [128, 8, VW], F32)
        stripb = sb("stripb", [128, SW], BF16)
        ss = sb("ss", [128, 16], F32)
        st4 = sb("st4", [128, 32], F32)
        modF = sb("modF", [128, 4, 8, NJ], F32)
        A12 = sb("A12", [128, 2, 8, NJ], F32)
        gT = sb("gT", [128, 16], F32)
        bmT = sb("bmT", [128, 48], F32)
        cwT = sb("cwT", [128, 4, NFC], F32)
        lamc = sb("lamc", [128, 4], F32)
        btab = sb("btab", [128, 24], F32)
        halo = sb("halo", [128, 2, 4, NFC], F32)
        apend = sb("apend", [128, NFC], BF16)
        pendz = sb("pendz", [128, NFC], F32)
        identb = sb("identb", [128, 128], BF16)
        identf = sb("identf", [128, 128], F32)
        Jf = sb("Jf", [128, 128], F32)
        cTs = sb("cTs", [128, 8, NJ], F32)
        scs = sb("scs", [128, 8, NJ], F32)
        g8 = sb("g8", [16, 128], F32)
        c88 = sb("c88", [88, 128], F32)
        bm48 = sb("bm48", [48, 128], F32)
        rb32 = sb("rb32", [32, 8], F32)
        pb = nc.alloc_psum_tensor("pb", [128, 8, 512], F32)

        def BK(i):
            return ("B", i)

        def pbT(i):
            return pb[:, i, :].bitcast(BF16)

        def bigk(off, n):
            return [("big", i) for i in range(off // 512, (off + n + 511) // 512)]

        def V(fn, r=(), w=()):
            return S.op("dve", fn, r, w)

        def A(fn, r=(), w=()):
            return S.op("act", fn, r, w)

        def P(fn, r=(), w=()):
            return S.op("pe", fn, r, w)

        def G(fn, r=(), w=()):
            return S.op("pool", fn, r, w)

        def LD(fn, r, w, sem):
            return S.op("sp", fn, r, w, dma=sem)

        def STO(fn, r, w, sem):
            return S.op("pool", fn, r, w, dma=sem)

        QTA = big[:, 0:4096].rearrange("p (h n) -> p h n", h=8)
        QTB = big[:, 4096:8192].rearrange("p (h n) -> p h n", h=8)
        sg = big[:, 8192:16384].rearrange("p (t c) -> p t c", t=4)
        aT = big[:, 0:NFC * 512].rearrange("p (f n) -> p f n", f=NFC)
        KTst = big[:, 0:5120].rearrange("p (h n) -> p h n", h=10)
        Vst = big[:, 5120:5120 + 10 * 4 * VW].rearrange("p (h t c) -> p h t c", h=10, t=4)
        K_KTst = bigk(0, 5120)
        K_Vst = bigk(5120, 10 * 4 * VW)

        def cast(dst, src, r0, r1, key):
            STO(lambda e: e.dma_start(out=dst[r0:r1, :], in_=src[r0:r1, :]), [], [key], ("cast", key))

        for i in range(4):
            cast(w_in_b, w_in, i * 256, (i + 1) * 256, ("w_in_b", i))
        G(lambda e: e.memset(identb[:], 0.0), [], ["identb"])
        G(lambda e: e.affine_select(out=identb[:], in_=identb[:], compare_op=ALU.not_equal, fill=1.0, base=0,
                                    pattern=[[-1, 128]], channel_multiplier=1), ["identb"], ["identb"])
        G(lambda e: e.memset(identf[:], 0.0), [], ["identf"])
        G(lambda e: e.affine_select(out=identf[:], in_=identf[:], compare_op=ALU.not_equal, fill=1.0, base=0,
                                    pattern=[[-1, 128]], channel_multiplier=1), ["identf"], ["identf"])
        G(lambda e: e.memset(Jf[:], 0.0), [], ["Jf"])
        G(lambda e: e.affine_select(out=Jf[:], in_=Jf[:], compare_op=ALU.not_equal, fill=1.0, base=-127,
                                    pattern=[[1, 128]], channel_multiplier=1), ["Jf"], ["Jf"])
        for i in range(2):
            cast(w_ffn_b, w_ffn, i * 512, (i + 1) * 512, ("w_ffn_b", i))
        cast(w_out_b, w_out, 0, D, ("w_out_b", 0))
        cast(w_down_b, w_down, 0, DFF, ("w_down_b", 0))
        K_WIN = [("w_in_b", i) for i in range(4)]
        K_WFFN = [("w_ffn_b", i) for i in range(2)]
        K_WOUT = [("w_out_b", 0)]
        K_WDN = [("w_down_b", 0)]

        LD(lambda e: e.dma_start(out=cTs[:], in_=cT.rearrange("(k p) s -> p k s", p=128)), [], ["cTs"], "m0")
        LD(lambda e: e.dma_start(out=g8[0:8, :], in_=g_norm1.rearrange("o (k p) -> (o k) p", p=128)), [], ["g8"], "m1")
        LD(lambda e: e.dma_start(out=g8[8:16, :], in_=g_norm2.rearrange("o (k p) -> (o k) p", p=128)), [], ["g8"], "m1")
        LD(lambda e: e.dma_start(out=c88[:], in_=conv_wb.rearrange("j (c p) -> (j c) p", p=128)), [], ["c88"], "m2")
        LD(lambda e: e.dma_start(out=bm48[:], in_=b_mod.rearrange("o (c p) -> (o c) p", p=128)), [], ["bm48"], "m3")
        LD(lambda e: e.dma_start(out=rb32[:], in_=rel_bias), [], ["rb32"], "m4")
        LD(lambda e: e.dma_start(out=rtmp[0:32, 0:GL], in_=oh), [], ["rtmp"], "m5")
        LD(lambda e: e.dma_start(out=btab[:, 0:16], in_=rb_far.partition_broadcast(128)), [], ["btab"], "m6")
        V(lambda e: e.tensor_tensor(out=btab[:, 16:24], in0=btab[:, 8:16], in1=btab[:, 0:8], op=ALU.subtract), ["btab"], ["btab"])
        LD(lambda e: e.dma_start(out=gfbc[:], in_=g_final.partition_broadcast(128)), [], ["gfbc"], "m7")
        LD(lambda e: e.dma_start(out=gq_bc[:], in_=g_qnorm.partition_broadcast(128)), [], ["gq_bc"], "m8")
        LD(lambda e: e.dma_start(out=gk_bc[:], in_=g_knorm.partition_broadcast(128)), [], ["gk_bc"], "m9")
        LD(lambda e: e.dma_start(out=gsub8[:], in_=g_subln.partition_broadcast(128)), [], ["gsub8"], "m10")
        for (dst, src, key, sem) in ((gq_sw, g_qnorm, "gq_sw", "m11"), (gk_sw, g_knorm, "gk_sw", "m12")):
            for a in range(4):
                so = (a ^ 1) * 32
                LD(lambda e, dst=dst, src=src, a=a, so=so: e.dma_start(
                    out=dst[:, a * 32:(a + 1) * 32], in_=src[:, so:so + 32].partition_broadcast(128)), [], [key], sem)
        LD(lambda e: e.dma_start(out=wk[0][:, 0:256], in_=lam_in.rearrange("a (o d) -> o (a d)", o=1).partition_broadcast(128)),
           [], ["wk0"], "m13")

        V(lambda e: e.tensor_scalar(out=gsub8[:], in0=gsub8[:], scalar1=1.0 - LAMBDA_INIT, scalar2=0.0, op0=ALU.mult, op1=ALU.add),
          ["gsub8"], ["gsub8"])
        V(lambda e: e.tensor_tensor(out=wk[0][:, 256:320], in0=wk[0][:, 0:64], in1=wk[0][:, 64:128], op=ALU.mult), ["wk0"], ["wk0"])
        V(lambda e: e.tensor_tensor(out=wk[0][:, 320:384], in0=wk[0][:, 128:192], in1=wk[0][:, 192:256], op=ALU.mult), ["wk0"], ["wk0"])
        V(lambda e: e.tensor_reduce(out=lamc[:, 0:2], in_=wk[0][:, 256:384].rearrange("p (a d) -> p a d", a=2), axis=AX.X, op=ALU.add),
          ["wk0"], ["lamc"])
        A(lambda e: e.activation(out=lamc[:, 0:2], in_=lamc[:, 0:2], func=AF.Exp), ["lamc"], ["lamc"])
        V(lambda e: e.tensor_tensor(out=lamc[:, 2:3], in0=lamc[:, 1:2], in1=lamc[:, 0:1], op=ALU.subtract), ["lamc"], ["lamc"])
        V(lambda e: e.tensor_scalar(out=lamc[:, 3:4], in0=lamc[:, 2:3], scalar1=-LAMBDA_INIT, scalar2=0.0, op0=ALU.add, op1=ALU.add),
          ["lamc"], ["lamc"])
        NEGLAM = lamc[:, 3:4]

        P(lambda e: e.matmul(pb[:, 0, 0:16], lhsT=g8[0:16, :], rhs=identf[0:16, 0:16], start=True, stop=True), ["g8", "identf"], [BK(0)])
        V(lambda e: e.tensor_copy(out=gT[:], in_=pb[:, 0, 0:16]), [BK(0)], ["gT"])
        P(lambda e: e.matmul(pb[:, 1, 0:88], lhsT=c88[0:88, :], rhs=identf[0:88, 0:88], start=True, stop=True), ["c88", "identf"], [BK(1)])
        V(lambda e: e.tensor_copy(out=cwT[:].rearrange("p j c -> p (j c)"), in_=pb[:, 1, 0:88]), [BK(1)], ["cwT"])
        P(lambda e: e.matmul(pb[:, 2, 0:48], lhsT=bm48[0:48, :], rhs=identf[0:48, 0:48], start=True, stop=True), ["bm48", "identf"], [BK(2)])
        V(lambda e: e.tensor_copy(out=bmT[:], in_=pb[:, 2, 0:48]), [BK(2)], ["bmT"])

        A(lambda e: e.activation(out=scs[:], in_=cTs[:], func=AF.Silu), ["cTs"], ["scs"])
        LD(lambda e: e.dma_start(out=wk[1][0:NJ, :], in_=b_mod[:, 2 * D:3 * D].partition_broadcast(NJ)), [], ["wk1"], "m14")
        LD(lambda e: e.dma_start(out=wk[2][0:NJ, :], in_=b_mod[:, 5 * D:6 * D].partition_broadcast(NJ)), [], ["wk2"], "m15")
        kinds = {0: 0, 1: 1, 3: 2, 4: 3}
        li = 0
        for blk in range(6):
            for hf in range(2):
                slot = li % 2
                li += 1
                wm = xsl[slot][:].rearrange("p t d -> p (t d)").rearrange("p (k c) -> p k c", k=8)
                c0 = blk * D + hf * 512
                LD(lambda e, wm=wm, c0=c0: e.dma_start(out=wm, in_=w_mod[:, c0:c0 + 512].rearrange("(k p) c -> p k c", p=128)),
                   [], [("x", slot)], ("x", slot))
                if blk in kinds:
                    ki = kinds[blk]
                    for j in range(4):
                        col = (ki * 8 + hf * 4 + j) * NJ
                        for k in range(8):
                            P(lambda e, wm=wm, j=j, k=k, col=col: e.matmul(
                                pb[:, 6, col:col + NJ], lhsT=wm[:, k, j * 128:(j + 1) * 128], rhs=scs[:, k, :],
                                start=(k == 0), stop=(k == 7)), [("x", slot), "scs"], [BK(6)])
                else:
                    gi = 0 if blk == 2 else 1
                    for k in range(8):
                        P(lambda e, wm=wm, k=k: e.matmul(pb[0:NJ, 7, :], lhsT=scs[:, k, :], rhs=wm[:, k, :],
                                                         start=(k == 0), stop=(k == 7)), [("x", slot), "scs"], [BK(7)])
                    bsrc = wk[1] if gi == 0 else wk[2]
                    V(lambda e, gi=gi, hf=hf, bsrc=bsrc: e.tensor_tensor(
                        out=wk[3][0:NJ, hf * 512:(hf + 1) * 512], in0=pb[0:NJ, 7, :], in1=bsrc[0:NJ, hf * 512:(hf + 1) * 512], op=ALU.add),
                      [BK(7), "wk1", "wk2"], ["wk3"])
                    if hf == 1:
                        STO(lambda e, gi=gi: e.dma_start(out=modbc[:, gi * D:(gi + 1) * D], in_=wk[3][0:NJ, :]), ["wk3"],
                            ["modbc", "pst_serial"], "pst")
        for ki in range(4):
            blk = [0, 1, 3, 4][ki]
            V(lambda e, ki=ki, blk=blk: e.tensor_tensor(
                out=modF[:, ki, :, :], in0=pb[:, 6, ki * 8 * NJ:(ki + 1) * 8 * NJ].rearrange("p (a b) -> p a b", a=8),
                in1=bmT[:, blk * 8:(blk + 1) * 8].unsqueeze(2).broadcast_to([128, 8, NJ]), op=ALU.add), [BK(6), "bmT"], ["modF"])
        for a, ki in ((0, 1), (1, 3)):
            V(lambda e, a=a, ki=ki: e.tensor_scalar(out=A12[:, a, :, :], in0=modF[:, ki, :, :], scalar1=1.0, scalar2=0.0, op0=ALU.add, op1=ALU.add),
              ["modF"], ["A12"])
            V(lambda e, a=a: e.tensor_tensor(out=A12[:, a, :, :], in0=A12[:, a, :, :],
                                             in1=gT[:, a * 8:(a + 1) * 8].unsqueeze(2).broadcast_to([128, 8, NJ]), op=ALU.mult),
              ["A12", "gT"], ["A12"])

        for ci, (c0, cn) in enumerate(((0, 512), (512, 512), (1024, 256))):
            P(lambda e, c0=c0, cn=cn, ci=ci: e.matmul(pb[0:8, 3 + ci, 0:cn], lhsT=rb32[0:32, 0:8], rhs=rtmp[0:32, c0:c0 + cn],
                                                       start=True, stop=True), ["rb32", "rtmp"], [BK(3 + ci)])
            V(lambda e, c0=c0, cn=cn, ci=ci: e.tensor_copy(out=strip[0:8, c0:c0 + cn], in_=pb[0:8, 3 + ci, 0:cn]), [BK(3 + ci)], ["strip"])
        STO(lambda e: e.dma_start(out=Gs, in_=strip[0:8, :]), ["strip"], ["Gs", "pst_serial"], "pst")
        for h in range(8):
            hank = bass.AP(tensor=Gs.tensor, offset=h * GL, ap=[[1, 128], [1, SW]])
            LD(lambda e, hank=hank: e.dma_start(out=rtmp[:, 0:SW], in_=hank), ["Gs"], ["rtmp"], "m5")
            for ci, (c0, cn) in enumerate(((0, 512), (512, 512), (1024, 128))):
                bk = 3 + ci
                P(lambda e, c0=c0, cn=cn, bk=bk: e.matmul(pb[:, bk, 0:cn], lhsT=Jf[:], rhs=rtmp[:, c0:c0 + cn], start=True, stop=True),
                  ["Jf", "rtmp"], [BK(bk)])
                V(lambda e, c0=c0, cn=cn, bk=bk: e.tensor_copy(out=strip[:, c0:c0 + cn], in_=pb[:, bk, 0:cn]), [BK(bk)], ["strip"])
            V(lambda e, h=h: e.tensor_scalar(out=stripb[:], in0=strip[:, 0:SW], scalar1=btab[:, h:h + 1], scalar2=1.0 / SC_B,
                                             op0=ALU.subtract, op1=ALU.mult), ["strip", "btab"], ["stripb"])
            STO(lambda e, h=h: e.dma_start(out=stripsb[h], in_=stripb[:]), ["stripb"], ["stripsb", "pst_serial"], "pst")

        ring_i = [0]

        def wtile(Wb, wkeys, r0, nk, c0, ncol):
            slot = ring_i[0] % 4
            ring_i[0] += 1
            LD(lambda e: e.dma_start(out=ring[slot][:, 0:nk, 0:ncol],
                                     in_=Wb[r0:r0 + nk * 128, c0:c0 + ncol].rearrange("(k p) c -> p k c", p=128)),
               wkeys, [("ring", slot)], ("ring", slot))
            return ring[slot], ("ring", slot)

        bank_i = [0]

        def nbank(lst):
            b = lst[bank_i[0] % len(lst)]
            bank_i[0] += 1
            return b

        cp_i = [0]

        def evac(out_ap, in_ap, r, w):
            cp_i[0] += 1
            if cp_i[0] % 2:
                A(lambda e: e.activation(out=out_ap, in_=in_ap, func=AF.Copy), r, w)
            else:
                V(lambda e: e.tensor_copy(out=out_ap, in_=in_ap), r, w)

        def rstd_from(ssap, n, r, w, outap):
            V(lambda e: e.tensor_scalar(out=outap, in0=ssap, scalar1=1.0 / n, scalar2=EPS, op0=ALU.mult, op1=ALU.add), r, w)
            A(lambda e: e.activation(out=outap, in_=outap, func=AF.Sqrt), w, w)
            V(lambda e: e.reciprocal(out=outap, in_=outap), w, w)

        def norm_p1(xkey, xt, nt, stat, c0, k0, k1):
            for t in range(nt):
                A(lambda e, t=t: e.activation(out=wk[3][:], in_=xt[:, t, :], func=AF.Square, accum_out=stat[:, c0 + t:c0 + t + 1]),
                  [xkey], ["wk3", k0])
            rstd_from(stat[:, c0:c0 + nt], float(D), [k0], [k1], stat[:, c0 + 8:c0 + 8 + nt])
            for t in range(nt):
                V(lambda e, t=t: e.tensor_scalar(out=tokbf[:, t, :], in0=xt[:, t, :], scalar1=stat[:, c0 + 8 + t:c0 + 9 + t], scalar2=0.0,
                                                 op0=ALU.mult, op1=ALU.add), [xkey, k1], [("tok", t)])

        def norm_p2(a_idx, sh_ki, jidx, nt, ks, banks):
            N = nt * 128
            for k in ks:
                bk = banks[k % len(banks)]
                for t in range(nt):
                    P(lambda e, t=t, k=k, bk=bk: e.transpose(out=pbT(bk)[:, t * 128:(t + 1) * 128], in_=tokbf[:, t, k * 128:(k + 1) * 128],
                                                              identity=identb[:]), [("tok", t), "identb"], [BK(bk)])
                V(lambda e, k=k, bk=bk: e.tensor_scalar(out=featbf[:, k, 0:N], in0=pbT(bk)[:, 0:N], scalar1=A12[:, a_idx, k, jidx:jidx + 1],
                                                        scalar2=modF[:, sh_ki, k, jidx:jidx + 1], op0=ALU.mult, op1=ALU.add),
                  [BK(bk), "A12", "modF"], [("feat", k)])

        def norm_feat(xkey, xt, a_idx, sh_ki, jidx, nt):
            norm_p1(xkey, xt, nt, ss, 0, "ss", "ss8")
            norm_p2(a_idx, sh_ki, jidx, nt, range(8), [4, 5, 6, 7])

        FEAT = [("feat", k) for k in range(8)]
        TOK = [("tok", t) for t in range(4)]

        def rope_apply(src, srckeys, H, Cg, Sg, g_keys, outbf, outkeys, wa, wb, wc, wak, wbk, wck):
            n = H * 128
            A(lambda e: e.activation(out=wa[:, 0:n], in_=src, func=AF.Copy), srckeys, [wak])
            V(lambda e: e.tensor_tensor(out=wb[:, 0:n], in0=wa[:, 0:n], in1=wa[:, 0:n], op=ALU.mult), [wak], [wbk])
            V(lambda e: e.tensor_reduce(out=st4[:, 0:H], in_=wb[:, 0:n].rearrange("p (h d) -> p h d", h=H), axis=AX.X, op=ALU.add),
              [wbk], ["st4"])
            rstd_from(st4[:, 0:H], 128.0, ["st4"], ["st4b"], st4[:, 8:8 + H])
            V(lambda e: e.tensor_tensor(out=wb[:, 0:n].rearrange("p (h d) -> p h d", h=H), in0=wa[:, 0:n].rearrange("p (h d) -> p h d", h=H),
                                        in1=Cg.unsqueeze(1).broadcast_to([128, H, 128]), op=ALU.mult), [wak] + g_keys, [wbk])
            for bsel in range(2):
                for ax in range(2):
                    xin = wa[:, 0:n].rearrange("p (h r) -> p h r", h=H)[:, :, ax * 64 + (1 - bsel) * 32: ax * 64 + (1 - bsel) * 32 + 32]
                    xout = wc[:, 0:n].rearrange("p (h r) -> p h r", h=H)[:, :, ax * 64 + bsel * 32: ax * 64 + bsel * 32 + 32]
                    sgin = Sg[:, ax * 64 + bsel * 32: ax * 64 + bsel * 32 + 32].unsqueeze(1).broadcast_to([128, H, 32])
                    V(lambda e, xin=xin, xout=xout, sgin=sgin: e.tensor_tensor(out=xout, in0=xin, in1=sgin, op=ALU.mult),
                      [wak] + g_keys, [wck])
            V(lambda e: e.tensor_tensor(out=wb[:, 0:n], in0=wb[:, 0:n], in1=wc[:, 0:n], op=ALU.add), [wbk, wck], [wbk])
            V(lambda e: e.tensor_tensor(out=outbf.rearrange("p (h d) -> p h d", h=H), in0=wb[:, 0:n].rearrange("p (h d) -> p h d", h=H),
                                        in1=st4[:, 8:8 + H].unsqueeze(2).broadcast_to([128, H, 128]), op=ALU.mult),
              [wbk, "st4b"], outkeys)

        def rope_tables(gbc, gsw, gkeys, nt):
            V(lambda e: e.tensor_tensor(out=ropet[:, 0:nt, 0:128], in0=ropet[:, 0:nt, 0:128],
                                        in1=gbc[:].unsqueeze(1).broadcast_to([128, nt, 128]), op=ALU.mult), ["ropet"] + gkeys, ["ropet"])
            V(lambda e: e.tensor_tensor(out=ropet[:, 0:nt, 128:256], in0=ropet[:, 0:nt, 128:256],
                                        in1=gsw[:].unsqueeze(1).broadcast_to([128, nt, 128]), op=ALU.mult), ["ropet"] + gkeys, ["ropet"])

        jobs = []
        for j in range(NP):
            jobs.append(dict(kind="p", j=j, n_tot=n_p, n_q=n_p, halo=False, x=xp[j], rope=rope_p, y=yp[j]))
        jobs.append(dict(kind="s", j=NP, n_tot=n_s, n_q=n_own, halo=True, x=xs, rope=rope_s, y=ys))
        units = []
        for job in jobs:
            for ck in range(job["n_tot"] // 512):
                units.append(dict(kind="kv", job=job, tok0=ck * 512, nt=4, ck=ck))
            nqc = job["n_q"] // 512
            for c in range(nqc):
                units.append(dict(kind="q", job=job, tok0=c * 512, nt=4, c=c, first=(c == 0), last=(c == nqc - 1 and not job["halo"])))
            if job["halo"]:
                units.append(dict(kind="h", job=job, tok0=job["n_q"], nt=1, c=nqc, first=False, last=False))
        for i, u in enumerate(units):
            u["slot"] = i % 2
            u["next"] = units[i + 1] if i + 1 < len(units) else None

        def load_x(u):
            job, tok0, nt, slot = u["job"], u["tok0"], u["nt"], u["slot"]
            LD(lambda e: e.dma_start(out=xsl[slot][:, 0:nt, :], in_=job["x"][tok0:tok0 + nt * 128, :].rearrange("(t p) d -> p t d", p=128)),
               [], [("x", slot)], ("x", slot))

        def load_rope(u):
            job, tok0, nt = u["job"], u["tok0"], u["nt"]
            LD(lambda e: e.dma_start(out=ropet[:, 0:nt, :], in_=job["rope"][tok0:tok0 + nt * 128, :].rearrange("(t p) d -> p t d", p=128)),
               [], ["ropet"], "ropet")

        kv_i = [0]
        ALLB = [0, 1, 2, 3, 4, 5]
        ALL8 = [0, 1, 2, 3, 4, 5, 6, 7]

        def unit_kv(u):
            job, tok0, slot, ck = u["job"], u["tok0"], u["slot"], u["ck"]
            jidx = job["j"]
            xk = ("x", slot)
            xt = xsl[slot]
            load_rope(u)
            if not u.get("prenormed"):
                norm_feat(xk, xt, 0, 0, jidx, 4)
            rope_tables(gk_bc, gk_sw, ["gk_bc", "gk_sw"], 4)
            V(lambda e: e.memset(Vst[:, :, :, 128:VW], 1.0), [], K_Vst)
            W, wkey = wtile(w_in_b, K_WIN, 0, 8, C_KA, 512)
            for t in range(4):
                bk = nbank(ALLB)
                for k in range(8):
                    P(lambda e, t=t, k=k, bk=bk, W=W: e.matmul(pb[:, bk, :], lhsT=featbf[:, k, t * 128:(t + 1) * 128], rhs=W[:, k, :],
                                                               start=(k == 0), stop=(k == 7)), [("feat", k), wkey], [BK(bk)])
                A(lambda e, t=t, bk=bk: e.activation(out=Vst[:, 0:2, t, 0:128], in_=pb[:, bk, 256:512].rearrange("p (h d) -> p h d", h=2),
                                                     func=AF.Copy), [BK(bk)], K_Vst)
                rope_apply(pb[:, bk, 0:256], [BK(bk)], 2, ropet[:, t, 0:128], ropet[:, t, 128:256], ["ropet"],
                           qkb[:, t, 0:256], [("qkb", t)], wk[0], wk[1], wk[2], "wk0", "wk1", "wk2")

            def ka_transposes():
                for t in range(4):
                    for h2 in range(2):
                        P(lambda e, t=t, h2=h2: e.transpose(out=pbT(7)[:, h2 * 512 + t * 128: h2 * 512 + (t + 1) * 128],
                                                             in_=qkb[:, t, h2 * 128:(h2 + 1) * 128], identity=identb[:]),
                          [("qkb", t), "identb"], [BK(7)])
                evac(KTst[:, 0:2, :], pbT(7)[:, 0:1024].rearrange("p (h n) -> p h n", h=2), [BK(7)], K_KTst)
            if u.get("next") is not None:
                load_x(u["next"])
            for half in range(2):
                W, wkey = wtile(w_in_b, K_WIN, 0, 8, C_KB + half * 512, 512)
                for hh in range(4):
                    h = half * 4 + hh
                    bk = nbank(ALLB)
                    for k in range(8):
                        P(lambda e, hh=hh, k=k, bk=bk, W=W: e.matmul(pb[:, bk, :], lhsT=W[:, k, hh * 128:(hh + 1) * 128], rhs=featbf[:, k, :],
                                                                     start=(k == 0), stop=(k == 7)), [("feat", k), wkey], [BK(bk)])
                    evac(KTst[:, 2 + h, :], pb[:, bk, :], [BK(bk)], K_KTst)
                if half == 0:
                    ka_transposes()
            un = u.get("next")
            if un is not None:
                nslot = un["slot"]
                norm_p1(("x", nslot), xsl[nslot], un["nt"], st4, 4, "st4p0", "st4p1")
                un["prenormed"] = True
            for half in range(2):
                W, wkey = wtile(w_in_b, K_WIN, 0, 8, C_VB + half * 512, 512)
                for t in range(4):
                    bk = nbank(ALLB)
                    for k in range(8):
                        P(lambda e, t=t, k=k, bk=bk, W=W: e.matmul(pb[:, bk, :], lhsT=featbf[:, k, t * 128:(t + 1) * 128], rhs=W[:, k, :],
                                                                   start=(k == 0), stop=(k == 7)), [("feat", k), wkey], [BK(bk)])
                    evac(Vst[:, 2 + half * 4:6 + half * 4, t, 0:128], pb[:, bk, :].rearrange("p (h d) -> p h d", h=4), [BK(bk)], K_Vst)
            if un is not None:
                norm_p2(0, 0, un["job"]["j"], un["nt"], range(8), [4, 5, 6, 7])
            T0 = tok0 // 128
            STO(lambda e: e.dma_start(out=KTs[:, :, tok0:tok0 + 512].rearrange("h p n -> p h n"), in_=KTst), K_KTst,
                [("kvs", ck), "kvs_serial"], "kvst")
            STO(lambda e: e.dma_start(out=Vs[:, :, T0:T0 + 4, :].rearrange("h p t c -> p h (t c)"),
                                      in_=Vst.rearrange("p h t c -> p h (t c)")), K_Vst, [("kvs", ck), "kvs_serial"], "kvst")

        OPOS = [(4, 0), (4, VW), (4, 2 * VW), (5, 0), (5, VW), (5, 2 * VW), (6, 0), (6, VW)]

        def attention(u, NQ, nt):
            job = u["job"]
            q0 = u["tok0"]
            nblk = job["n_tot"] // KVB
            entries = []
            for h in range(8):
                for isB in (False, True):
                    hd = dict(isB=isB, h=h, touched=set(), scale=SC_B if isB else SC_A)
                    if not isB:
                        hd["Oacc"] = [[OPOS[j] for j in range(nt)]]
                    else:
                        hd["Oacc"] = [[OPOS[j] for j in range(nt)], [OPOS[4 + j] for j in range(nt)]]
                    for b in range(nblk):
                        entries.append(dict(hd=hd, hh=(2 + h) if isB else (h // 4), b=b, slot=kv_i[0] % 3))
                        kv_i[0] += 1

            def rec_load(k):
                if k >= len(entries):
                    return
                ent = entries[k]
                slot, b, hh = ent["slot"], ent["b"], ent["hh"]
                cks = [("kvs", c_) for c_ in range(b * KVB // 512, (b + 1) * KVB // 512)]
                LD(lambda e: e.dma_start(out=kvK[slot][:], in_=KTs[hh, :, b * KVB:(b + 1) * KVB]), cks, [("kvK", slot)], ("kvK", slot))
                LD(lambda e: e.dma_start(out=kvV[slot][:].rearrange("p t c -> p (t c)"),
                                         in_=Vs[hh, :, b * TB:(b + 1) * TB, :].rearrange("p t c -> p (t c)")),
                   cks, [("kvV", slot)], ("kvV", slot))

            items = []
            for k, ent in enumerate(entries):
                hd, slot, b = ent["hd"], ent["slot"], ent["b"]
                step = 1 if hd["isB"] else 2
                for tl in range(0, TB, step):
                    items.append(dict(hd=hd, slot=slot, tls=list(range(tl, tl + step)), s0=b * KVB + tl * 128,
                                      kfirst=(k if tl == 0 else None), hfirst=(b == 0 and tl == 0),
                                      hlast=(b == nblk - 1 and tl + step >= TB)))
            rec_load(0)

            def rec_qk(i):
                it = items[i]
                hd, slot, tls = it["hd"], it["slot"], it["tls"]
                h = hd["h"]
                if it["kfirst"] is not None:
                    rec_load(it["kfirst"] + 1)
                if it["hfirst"] and hd["isB"]:
                    LD(lambda e: e.dma_start(out=stripb[:], in_=stripsb[h]), ["stripsb"], ["stripb"], "stripb")
                sb0 = (i % 2) * 2
                if hd["isB"]:
                    tl = tls[0]
                    delta = it["s0"] - q0
                    band = (delta - (NQ - 1) < 91) and (delta + 127 > -91)
                    it["band"] = band
                    P(lambda e: e.matmul(pb[:, sb0, 0:NQ], lhsT=kvK[slot][0:64, tl * 128:(tl + 1) * 128], rhs=QTB[0:64, h, 0:NQ],
                                         start=True, stop=not band), [("kvK", slot)] + bigk(4096 + h * 512, 512), [BK(sb0)])
                    P(lambda e: e.matmul(pb[:, sb0 + 1, 0:NQ], lhsT=kvK[slot][64:128, tl * 128:(tl + 1) * 128], rhs=QTB[64:128, h, 0:NQ],
                                         start=True, stop=not band, tile_position=(64, 0)), [("kvK", slot)] + bigk(4096 + h * 512, 512), [BK(sb0 + 1)])
                    if band:
                        off = 512 - delta
                        for ub in range(2):
                            P(lambda e, ub=ub: e.matmul(pb[:, sb0 + ub, 0:NQ], lhsT=identb[:], rhs=stripb[:, off:off + NQ],
                                                        start=False, stop=True), ["identb", "stripb"], [BK(sb0 + ub)])
                else:
                    for ui, tl in enumerate(tls):
                        P(lambda e, ui=ui, tl=tl: e.matmul(pb[:, sb0 + ui, 0:NQ], lhsT=kvK[slot][:, tl * 128:(tl + 1) * 128], rhs=QTA[:, h, 0:NQ],
                                                           start=True, stop=True), [("kvK", slot)] + bigk(h * 512, 512), [BK(sb0 + ui)])

            def rec_exp(i):
                it = items[i]
                hd, s0 = it["hd"], it["s0"]
                h, scale = hd["h"], hd["scale"]
                sb0 = (i % 2) * 2
                pt = PT[i % 2]
                ptk = "pt%d" % (i % 2)
                src = pb[:, sb0:sb0 + 2, 0:NQ]
                if hd["isB"]:
                    delta = s0 - q0
                    if it["band"]:
                        A(lambda e: e.activation(out=pt[:, :, 0:NQ], in_=src, func=AF.Exp, scale=scale), [BK(sb0), BK(sb0 + 1)], [ptk])
                    elif delta < 0:
                        A(lambda e: e.activation(out=pt[:, :, 0:NQ], in_=src, func=AF.Exp, scale=scale), [BK(sb0), BK(sb0 + 1)], [ptk])
                    else:
                        bcol = 16 + h
                        A(lambda e: e.activation(out=pt[:, :, 0:NQ], in_=src, func=AF.Exp, bias=btab[:, bcol:bcol + 1], scale=scale),
                          [BK(sb0), BK(sb0 + 1), "btab"], [ptk])
                else:
                    A(lambda e: e.activation(out=pt[:, :, 0:NQ], in_=src, func=AF.Exp, scale=scale), [BK(sb0), BK(sb0 + 1)], [ptk])

            def rec_pv(i):
                it = items[i]
                hd, slot, tls = it["hd"], it["slot"], it["tls"]
                isB = hd["isB"]
                pt = PT[i % 2]
                ptk = "pt%d" % (i % 2)
                for uu in range(2):
                    tl = tls[0] if isB else tls[uu]
                    acc = hd["Oacc"][uu] if isB else hd["Oacc"][0]
                    for j in range(nt):
                        bk, col = acc[j]
                        first = bk not in hd["touched"]
                        hd["touched"].add(bk)
                        P(lambda e, uu=uu, tl=tl, j=j, bk=bk, col=col, first=first: e.matmul(
                            pb[:, bk, col:col + VW], lhsT=pt[:, uu, j * 128:(j + 1) * 128], rhs=kvV[slot][:, tl, :],
                            start=first, stop=False, skip_group_check=True), [ptk, ("kvV", slot)], [BK(bk)])

            pend_ep = []

            def tick_ep(force=False):
                while pend_ep and (force or pend_ep[0][0] <= 0):
                    pend_ep.pop(0)[1]()
                for pe_ in pend_ep:
                    pe_[0] -= 1

            def epilogue(hd):
                tick_ep(force=True)
                isB, h = hd["isB"], hd["h"]
                banks = sorted(hd["touched"])
                nin = {4: 3, 5: 3, 6: 2}
                for bk in banks:
                    na = nin[bk] if (isB or bk == 4) else 1
                    i0_ = (bk - 4) * 3
                    V(lambda e, bk=bk, na=na, i0_=i0_: e.tensor_copy(out=osb[:, i0_:i0_ + na, :].rearrange("p a c -> p (a c)"),
                                                                       in_=pb[:, bk, 0:na * VW]), [BK(bk)], [("osb", bk)])
                OK = [("osb", bk) for bk in banks]
                rb = lambda c0: st4[:, c0:c0 + nt].unsqueeze(2).broadcast_to([128, nt, 128])
                if not isB:
                    V(lambda e: e.reciprocal(out=st4[:, 16:16 + nt], in_=osb[:, 0:nt, 128]), OK, ["st4c"])
                    V(lambda e: e.tensor_tensor(out=oa[:, 0:nt, :], in0=osb[:, 0:nt, 0:128], in1=rb(16), op=ALU.mult), OK + ["st4c"], ["oa"])
                    V(lambda e: e.tensor_tensor(out=oa[:, 0:nt, :], in0=oa[:, 0:nt, :], in1=sg[:, 0:nt, h * 128:(h + 1) * 128], op=ALU.mult),
                      ["oa"] + bigk(8192, 8192), ["oa"])
                    return
                t1 = wk[2][:, 0:512].rearrange("p (j d) -> p j d", j=4)
                ob = wk[3][:, 0:512].rearrange("p (j d) -> p j d", j=4)
                sq = wk[2][:, 512:1024].rearrange("p (j d) -> p j d", j=4)
                V(lambda e: e.reciprocal(out=st4[:, 16:16 + nt], in_=osb[:, 0:nt, 128]), OK, ["st4c"])
                V(lambda e: e.reciprocal(out=st4[:, 20:20 + nt], in_=osb[:, 4:4 + nt, 128]), OK, ["st4e"])
                V(lambda e: e.tensor_scalar(out=st4[:, 20:20 + nt], in0=st4[:, 20:20 + nt], scalar1=NEGLAM, scalar2=0.0, op0=ALU.mult, op1=ALU.add),
                  ["st4e", "lamc"], ["st4e"])
                V(lambda e: e.tensor_tensor(out=t1[:, 0:nt, :], in0=osb[:, 0:nt, 0:128], in1=rb(16), op=ALU.mult), OK + ["st4c"], ["wk2"])
                V(lambda e: e.tensor_tensor(out=ob[:, 0:nt, :], in0=osb[:, 4:4 + nt, 0:128], in1=rb(20), op=ALU.mult), OK + ["st4e"], ["wk3"])
                V(lambda e: e.tensor_tensor(out=ob[:, 0:nt, :], in0=ob[:, 0:nt, :], in1=t1[:, 0:nt, :], op=ALU.add), ["wk3", "wk2"], ["wk3"])
                V(lambda e: e.tensor_tensor(out=sq[:, 0:nt, :], in0=ob[:, 0:nt, :], in1=ob[:, 0:nt, :], op=ALU.mult), ["wk3"], ["wk2"])
                V(lambda e: e.tensor_reduce(out=st4[:, 24:24 + nt], in_=sq[:, 0:nt, :], axis=AX.X, op=ALU.add), ["wk2"], ["st4d"])
                V(lambda e: e.tensor_scalar(out=st4[:, 24:24 + nt], in0=st4[:, 24:24 + nt], scalar1=1.0 / 128, scalar2=EPS, op0=ALU.mult, op1=ALU.add),
                  ["st4d"], ["st4d"])

                def stage23():
                    A(lambda e: e.activation(out=st4[:, 24:24 + nt], in_=st4[:, 24:24 + nt], func=AF.Ln), ["st4d"], ["st4d"])
                    A(lambda e: e.activation(out=st4[:, 24:24 + nt], in_=st4[:, 24:24 + nt], func=AF.Exp, scale=-0.5), ["st4d"], ["st4d"])
                    V(lambda e: e.tensor_tensor(out=ob[:, 0:nt, :], in0=ob[:, 0:nt, :], in1=rb(24), op=ALU.mult), ["wk3", "st4d"], ["wk3"])
                    V(lambda e: e.tensor_tensor(out=ob[:, 0:nt, :], in0=ob[:, 0:nt, :], in1=sg[:, 0:nt, D + h * 128:D + (h + 1) * 128], op=ALU.mult),
                      ["wk3"] + bigk(8192, 8192), ["wk3"])
                    V(lambda e: e.tensor_tensor(out=tokbf[:, 0:nt, h * 128:(h + 1) * 128], in0=ob[:, 0:nt, :], in1=oa[:, 0:nt, :], op=ALU.add),
                      ["wk3", "oa"], TOK)
                pend_ep.append([5, stage23])

            n_it = len(items)
            for i in range(n_it + 1):
                if i < n_it:
                    rec_qk(i)
                    rec_exp(i)
                if i >= 1:
                    rec_pv(i - 1)
                    if items[i - 1]["hlast"]:
                        epilogue(items[i - 1]["hd"])
                tick_ep()
            tick_ep(force=True)


        def unit_q(u):
            job, tok0, slot, nt, c = u["job"], u["tok0"], u["slot"], u["nt"], u["c"]
            jidx = job["j"]
            par = c % 2
            NQ = nt * 128
            xk = ("x", slot)
            xt = xsl[slot]
            is_halo = (u["kind"] == "h")
            load_rope(u)
            if c == 0:
                LD(lambda e: e.dma_start(out=gtbc[:], in_=modbc[jidx:jidx + 1, :].partition_broadcast(128)), ["modbc"], ["gtbc"], "gtbc")
            if not u.get("prenormed"):
                norm_feat(xk, xt, 0, 0, jidx, nt)
            rope_tables(gq_bc, gq_sw, ["gq_bc", "gq_sw"], nt)
            ropeq = []
            gi_ = 0
            for half in range(2):
                W, wkey = wtile(w_in_b, K_WIN, 0, 8, C_QA + half * 512, 512)
                for t in range(nt):
                    bk = nbank(ALLB)
                    for k in range(8):
                        P(lambda e, t=t, k=k, bk=bk, W=W: e.matmul(pb[:, bk, :], lhsT=featbf[:, k, t * 128:(t + 1) * 128], rhs=W[:, k, :],
                                                                   start=(k == 0), stop=(k == 7)), [("feat", k), wkey], [BK(bk)])
                    qi = gi_
                    gi_ += 1
                    qbuf = tokbf[:, qi // 2, (qi % 2) * 512:(qi % 2) * 512 + 512]
                    qkey = ("tok", qi // 2)
                    wa_i = 0 if qi % 2 == 0 else 3
                    rope_apply(pb[:, bk, :], [BK(bk)], 4, ropet[:, t, 0:128], ropet[:, t, 128:256], ["ropet"],
                               qbuf, [qkey], wk[wa_i], wk[1], wk[2], "wk%d" % wa_i, "wk1", "wk2")

                    def do_tr(t=t, half=half, qbuf=qbuf, qkey=qkey):
                        for hq in range(4):
                            tb = 6 + hq // 2
                            P(lambda e, t=t, hq=hq, tb=tb, qbuf=qbuf: e.transpose(
                                out=pbT(tb)[:, (hq % 2) * 512 + t * 128:(hq % 2) * 512 + (t + 1) * 128],
                                in_=qbuf[:, hq * 128:(hq + 1) * 128], identity=identb[:]), [qkey, "identb"], [BK(tb)])
                        if t == nt - 1:
                            for pr in range(2):
                                h0 = half * 4 + pr * 2
                                evac(QTA[:, h0:h0 + 2, 0:NQ], pbT(6 + pr)[:, 0:1024].rearrange("p (h n) -> p h n", h=2)[:, :, 0:NQ], [BK(6 + pr)],
                                     bigk(h0 * 512, 1024))
                    ropeq.append(do_tr)
            for _ in range(min(2, len(ropeq))):
                ropeq.pop(0)()
            mcount = [0]

            def after_group():
                if ropeq and mcount[0] % 3 == 0:
                    ropeq.pop(0)()
                mcount[0] += 1

            for gi in range(4):
                W, wkey = wtile(w_in_b, K_WIN, 0, 8, C_GA + gi * 512, 512)
                for t in range(nt):
                    bk = nbank(ALLB)
                    for k in range(8):
                        P(lambda e, t=t, k=k, bk=bk, W=W: e.matmul(pb[:, bk, :], lhsT=featbf[:, k, t * 128:(t + 1) * 128], rhs=W[:, k, :],
                                                                   start=(k == 0), stop=(k == 7)), [("feat", k), wkey], [BK(bk)])
                    A(lambda e, t=t, bk=bk, gi=gi: e.activation(out=sg[:, t, gi * 512:(gi + 1) * 512], in_=pb[:, bk, :], func=AF.Sigmoid),
                      [BK(bk)], bigk(8192 + t * 2048 + gi * 512, 512))
                    after_group()
            for t in range(nt):
                V(lambda e, t=t: e.tensor_tensor(out=sg[:, t, D:2 * D].rearrange("p (h d) -> p h d", h=8),
                                                 in0=sg[:, t, D:2 * D].rearrange("p (h d) -> p h d", h=8),
                                                 in1=gsub8[:].unsqueeze(1).broadcast_to([128, 8, 128]), op=ALU.mult),
                  bigk(8192 + t * 2048 + D, D) + ["gsub8"], bigk(8192 + t * 2048 + D, D))
            for half in range(2):
                W, wkey = wtile(w_in_b, K_WIN, 0, 8, C_QB + half * 512, 512)
                for hq in range(4):
                    h = half * 4 + hq
                    bk = nbank(ALLB)
                    for k in range(8):
                        P(lambda e, hq=hq, k=k, bk=bk, W=W: e.matmul(pb[:, bk, 0:NQ], lhsT=W[:, k, hq * 128:(hq + 1) * 128], rhs=featbf[:, k, 0:NQ],
                                                                     start=(k == 0), stop=(k == 7)), [("feat", k), wkey], [BK(bk)])
                    evac(QTB[:, h, 0:NQ], pb[:, bk, 0:NQ], [BK(bk)], bigk(4096 + h * 512, 512))
                    after_group()
            while ropeq:
                ropeq.pop(0)()
            if u.get("next") is not None:
                load_x(u["next"])
            attention(u, NQ, nt)
            for k in range(8):
                bk = 4 + (k % 4)
                for t in range(nt):
                    P(lambda e, t=t, k=k, bk=bk: e.transpose(out=pbT(bk)[:, t * 128:(t + 1) * 128], in_=tokbf[:, t, k * 128:(k + 1) * 128],
                                                              identity=identb[:]), [("tok", t), "identb"], [BK(bk)])
                evac(featbf[:, k, 0:NQ], pbT(bk)[:, 0:NQ], [BK(bk)], [("feat", k)])
            for half in range(2):
                W, wkey = wtile(w_out_b, K_WOUT, 0, 8, half * 512, 512)
                for t in range(nt):
                    bk = nbank(ALL8)
                    for k in range(8):
                        P(lambda e, t=t, k=k, bk=bk, W=W: e.matmul(pb[:, bk, :], lhsT=featbf[:, k, t * 128:(t + 1) * 128], rhs=W[:, k, :],
                                                                   start=(k == 0), stop=(k == 7)), [("feat", k), wkey], [BK(bk)])
                    wt = wk[t % 2]
                    wtk = "wk%d" % (t % 2)
                    V(lambda e, bk=bk, half=half, wt=wt: e.tensor_tensor(out=wt[:, 0:512], in0=pb[:, bk, :], in1=gtbc[:, half * 512:(half + 1) * 512],
                                                                         op=ALU.mult), [BK(bk), "gtbc"], [wtk])
                    V(lambda e, t=t, half=half, wt=wt: e.tensor_tensor(out=xt[:, t, half * 512:(half + 1) * 512],
                                                                       in0=xt[:, t, half * 512:(half + 1) * 512], in1=wt[:, 0:512], op=ALU.add),
                      [xk, wtk], [xk])
            if not is_halo:
                STO(lambda e: e.dma_start(out=rtmp[0:1, par * D:(par + 1) * D], in_=xt[127:128, 3, :]), [xk], [("pendrow", par), "rtmp"], ("pendrow", par))
            norm_feat(xk, xt, 1, 2, jidx, nt)
            hprev = halo[:, 1 - par]
            hcur = halo[:, par]
            HP = ("halo", 1 - par)
            HC = ("halo", par)
            W = None
            for fc in range(NFC):
                if fc % 4 == 0:
                    ncol = min(512, DFF - fc * 128)
                    W, wkey = wtile(w_ffn_b, K_WFFN, 0, 8, DFF + fc * 128, ncol)
                bk = nbank(ALL8)
                fo = (fc % 4) * 128
                for k in range(8):
                    P(lambda e, k=k, bk=bk, W=W, fo=fo: e.matmul(pb[:, bk, 0:NQ], lhsT=W[:, k, fo:fo + 128], rhs=featbf[:, k, 0:NQ],
                                                                 start=(k == 0), stop=(k == 7)), [("feat", k), wkey], [BK(bk)])
                Gp = pb[:, bk, :]
                A(lambda e, fc=fc, Gp=Gp: e.activation(out=hcur[:, 3, fc:fc + 1], in_=Gp[:, 0:1], func=AF.Copy), [BK(bk)], [HC])
                if is_halo:
                    continue
                acc = wk[fc % 2]
                ak = "wk%d" % (fc % 2)
                A(lambda e, fc=fc, Gp=Gp, acc=acc: e.activation(out=acc[:, 0:NQ], in_=Gp[:, 0:NQ], func=AF.Identity,
                                                                bias=cwT[:, 3, fc:fc + 1], scale=cwT[:, 1, fc:fc + 1]), [BK(bk), "cwT"], [ak])
                V(lambda e, fc=fc, Gp=Gp, acc=acc: e.scalar_tensor_tensor(out=acc[:, 1:NQ], in0=Gp[:, 0:NQ - 1], scalar=cwT[:, 0, fc:fc + 1],
                                                                        in1=acc[:, 1:NQ], op0=ALU.mult, op1=ALU.add), [BK(bk), "cwT", ak], [ak])
                V(lambda e, fc=fc, Gp=Gp, acc=acc: e.scalar_tensor_tensor(out=acc[:, 0:NQ - 1], in0=Gp[:, 1:NQ], scalar=cwT[:, 2, fc:fc + 1],
                                                                        in1=acc[:, 0:NQ - 1], op0=ALU.mult, op1=ALU.add), [BK(bk), "cwT", ak], [ak])
                if not u["first"]:
                    V(lambda e, fc=fc, acc=acc: e.scalar_tensor_tensor(out=acc[:, 0:1], in0=hprev[:, 0, fc:fc + 1], scalar=cwT[:, 0, fc:fc + 1],
                                                                       in1=acc[:, 0:1], op0=ALU.mult, op1=ALU.add), [HP, "cwT", ak], [ak])
                A(lambda e, fc=fc, Gp=Gp: e.activation(out=hcur[:, 0, fc:fc + 1], in_=Gp[:, NQ - 1:NQ], func=AF.Copy), [BK(bk)], [HC])
                A(lambda e, fc=fc, acc=acc: e.activation(out=hcur[:, 1, fc:fc + 1], in_=acc[:, NQ - 1:NQ], func=AF.Copy), [ak], [HC])
                A(lambda e, fc=fc, acc=acc: e.activation(out=aT[:, fc, 0:NQ], in_=acc[:, 0:NQ], func=AF.Gelu), [ak], bigk(fc * 512, 512))
            if not is_halo:
                for fc in range(NFC):
                    if fc % 4 == 0:
                        ncol = min(512, DFF - fc * 128)
                        W, wkey = wtile(w_ffn_b, K_WFFN, 0, 8, fc * 128, ncol)
                    bk = nbank(ALL8)
                    fo = (fc % 4) * 128
                    for k in range(8):
                        P(lambda e, k=k, bk=bk, W=W, fo=fo: e.matmul(pb[:, bk, 0:NQ], lhsT=W[:, k, fo:fo + 128], rhs=featbf[:, k, 0:NQ],
                                                                     start=(k == 0), stop=(k == 7)), [("feat", k), wkey], [BK(bk)])
                    A(lambda e, fc=fc, bk=bk: e.activation(out=hcur[:, 2, fc:fc + 1], in_=pb[:, bk, NQ - 1:NQ], func=AF.Copy), [BK(bk)], [HC])
                    V(lambda e, fc=fc, bk=bk: e.tensor_tensor(out=aT[:, fc, 0:NQ], in0=pb[:, bk, 0:NQ], in1=aT[:, fc, 0:NQ], op=ALU.mult),
                      [BK(bk)] + bigk(fc * 512, 512), bigk(fc * 512, 512))
            have_pend = not u["first"]
            if have_pend:
                pend_prepare(hprev, HP, hcur[:, 3, :], [HC])
            hook = None
            un = u.get("next")
            if un is not None and (have_pend or not is_halo):
                nslot = un["slot"]
                norm_p1(("x", nslot), xsl[nslot], un["nt"], st4, 4, "st4p0", "st4p1")
                un["prenormed"] = True
                ksets = [[0, 1, 2], [3, 4, 5], [6, 7]]

                def hook(ti, un=un):
                    if ti < 3:
                        norm_p2(0, 0, un["job"]["j"], un["nt"], ksets[ti], [5, 6, 7])
            down_proj(u, xk, xt, 1 - par, have_pend, do_main=not is_halo, hook=hook)
            if u["last"]:
                pend_prepare(hcur, HC, None, [])
                down_proj(dict(u, tok0=tok0 + 512), xk, xt, par, True, do_main=False)

        def pend_prepare(hp, hpk, gfirst, gkeys):
            if gfirst is not None:
                V(lambda e: e.tensor_tensor(out=pendz[:], in0=cwT[:, 2, :], in1=gfirst, op=ALU.mult), ["cwT"] + gkeys, ["pendz"])
                V(lambda e: e.tensor_tensor(out=pendz[:], in0=pendz[:], in1=hp[:, 1, :], op=ALU.add), ["pendz", hpk], ["pendz"])
            else:
                V(lambda e: e.tensor_copy(out=pendz[:], in_=hp[:, 1, :]), [hpk], ["pendz"])
            A(lambda e: e.activation(out=pendz[:], in_=pendz[:], func=AF.Gelu), ["pendz"], ["pendz"])
            V(lambda e: e.tensor_tensor(out=apend[:], in0=pendz[:], in1=hp[:, 2, :], op=ALU.mult), ["pendz", hpk], ["apend"])

        def down_proj(u, xk, xt, ppar, have_pend, do_main, hook=None):
            job, tok0, nt, slot = u["job"], u["tok0"], u["nt"], u["slot"]
            NQ = nt * 128
            groups = [(0, 8), (8, 8), (16, 6)]
            for half in range(2):
                banks = [0, 1, 2, 3] if half == 0 else [5, 6, 7, 0]
                pbk = 4 if half == 0 else 1
                for gi, (f0, nf) in enumerate(groups):
                    W, wkey = wtile(w_down_b, K_WDN, f0 * 128, nf, half * 512, 512)
                    for fl in range(nf):
                        fc = f0 + fl
                        st = (fc == 0)
                        sp_ = (fc == NFC - 1)
                        if do_main:
                            for t in range(nt):
                                P(lambda e, t=t, fc=fc, fl=fl, W=W, st=st, sp_=sp_, bk=banks[t]: e.matmul(
                                    pb[:, bk, :], lhsT=aT[:, fc, t * 128:(t + 1) * 128], rhs=W[:, fl, :], start=st, stop=sp_),
                                  bigk(fc * 512, 512) + [wkey], [BK(banks[t])])
                        if have_pend:
                            P(lambda e, fc=fc, fl=fl, W=W, st=st, sp_=sp_, pbk=pbk: e.matmul(
                                pb[0:1, pbk, :], lhsT=apend[:, fc:fc + 1], rhs=W[:, fl, :], start=st, stop=sp_), ["apend", wkey], [BK(pbk)])
                    if hook is not None and half == 0:
                        hook(gi)
                if do_main:
                    for t in range(nt):
                        bk = banks[t]
                        wt = wk[t % 2]
                        wtk = "wk%d" % (t % 2)
                        V(lambda e, bk=bk, half=half, wt=wt: e.tensor_tensor(out=wt[:, 0:512], in0=pb[:, bk, :],
                                                                             in1=gtbc[:, D + half * 512:D + (half + 1) * 512], op=ALU.mult),
                          [BK(bk), "gtbc"], [wtk])
                        V(lambda e, t=t, half=half, wt=wt: e.tensor_tensor(out=xt[:, t, half * 512:(half + 1) * 512],
                                                                           in0=xt[:, t, half * 512:(half + 1) * 512], in1=wt[:, 0:512], op=ALU.add),
                          [xk, wtk], [xk])
                if have_pend:
                    V(lambda e, half=half, pbk=pbk: e.tensor_tensor(out=wk[3][0:1, half * 512:(half + 1) * 512], in0=pb[0:1, pbk, :],
                                                           in1=gtbc[0:1, D + half * 512:D + (half + 1) * 512], op=ALU.mult),
                      [BK(pbk), "gtbc"], ["wk3"])
            if have_pend:
                V(lambda e: e.tensor_tensor(out=wk[3][0:1, :], in0=wk[3][0:1, :], in1=rtmp[0:1, ppar * D:(ppar + 1) * D], op=ALU.add),
                  ["wk3", ("pendrow", ppar)], ["wk3"])
                A(lambda e: e.activation(out=wk[2][0:1, :], in_=wk[3][0:1, :], func=AF.Square, accum_out=ss[0:1, 12:13]), ["wk3"], ["wk2", "ss12"])
                rstd_from(ss[0:1, 12:13], float(D), ["ss12"], ["ss13"], ss[0:1, 13:14])
                V(lambda e: e.scalar_tensor_tensor(out=wk[3][0:1, :], in0=wk[3][0:1, :], scalar=ss[0:1, 13:14], in1=gfbc[0:1, :],
                                                   op0=ALU.mult, op1=ALU.mult), ["wk3", "ss13", "gfbc"], ["wk3"])
                STO(lambda e: e.dma_start(out=job["y"][tok0 - 1:tok0, :], in_=wk[3][0:1, :]), ["wk3"],
                    [("y", job["j"], (tok0 - 1) // 512)], "ypend")
            if do_main:
                for t in range(nt):
                    A(lambda e, t=t: e.activation(out=wk[3][:], in_=xt[:, t, :], func=AF.Square, accum_out=ss[:, t:t + 1]), [xk], ["wk3", "ss"])
                rstd_from(ss[:, 0:nt], float(D), ["ss"], ["ss8"], ss[:, 8:8 + nt])
                for t in range(nt):
                    V(lambda e, t=t: e.scalar_tensor_tensor(out=xt[:, t, :], in0=xt[:, t, :], scalar=ss[:, 8 + t:9 + t], in1=gfbc[:],
                                                            op0=ALU.mult, op1=ALU.mult), [xk, "ss8", "gfbc"], [xk])
                STO(lambda e: e.dma_start(out=job["y"][tok0:tok0 + NQ, :].rearrange("(t p) d -> p t d", p=128), in_=xt[:, 0:nt, :]),
                    [xk], [("y", job["j"], tok0 // 512)], ("yst", slot))

        if units:
            load_x(units[0])
        for i, u in enumerate(units):
            if u["kind"] == "kv":
                unit_kv(u)
            else:
                unit_q(u)
        S.emit()
    return nc


def _t5_bucket_np(rel):
    nb = 16
    ret = (rel > 0).astype(np.int32) * nb
    n = np.abs(rel)
    max_exact = nb // 2
    nf = np.maximum(n, 1).astype(np.float32)
    large = max_exact + (np.log(nf / np.float32(max_exact)) / np.float32(math.log(128 / max_exact))
                         * np.float32(nb - max_exact)).astype(np.int32)
    large = np.minimum(large, nb - 1)
    return ret + np.where(n < max_exact, n, large)


def _rope_table(pos):
    pos = np.asarray(pos)
    row = (pos // 64).astype(np.float32)
    col = (pos % 64).astype(np.float32)
    inv = (np.float32(10000.0) ** (-np.arange(0, 64, 2, dtype=np.float32) / np.float32(64))).astype(np.float32)
    ar = row[:, None] * inv[None, :]
    ac = col[:, None] * inv[None, :]
    cr, sr, cc, sc = np.cos(ar), np.sin(ar), np.cos(ac), np.sin(ac)
    C = np.concatenate([cr, cr, cc, cc], axis=1)
    Sn = np.concatenate([-sr, sr, -sc, sc], axis=1)
    return np.ascontiguousarray(np.concatenate([C, Sn], axis=1).astype(np.float32))


def _core_inputs(rev, xp_list, xsamp, own_lo, own_hi, c_list, W):
    n_p = xp_list[0].shape[0]
    n_s = xsamp.shape[0]
    if not rev:
        pos_p = np.arange(n_p)
        pos_s = np.arange(n_s)
    else:
        pos_p = np.arange(n_p)[::-1]
        pos_s = np.arange(n_s)[::-1]
    m = {}
    m["xp"] = np.ascontiguousarray(np.stack([x[pos_p] for x in xp_list]))
    m["xs"] = np.ascontiguousarray(xsamp[pos_s])
    m["cT"] = np.ascontiguousarray(np.stack(c_list, axis=1))
    m["rope_p"] = _rope_table(pos_p)
    m["rope_s"] = _rope_table(pos_s)
    mm = np.arange(GL)
    rel = 639 - mm
    if rev:
        rel = -rel
    bk = _t5_bucket_np(rel)
    ohm = np.zeros((32, GL), np.float32)
    ohm[bk, mm] = 1.0
    m["oh"] = ohm
    rb = W["rel_bias"]
    if not rev:
        m["rb_far"] = np.ascontiguousarray(np.concatenate([rb[15], rb[31]])[None, :])
    else:
        m["rb_far"] = np.ascontiguousarray(np.concatenate([rb[31], rb[15]])[None, :])
    cw = W["conv_w"][0]
    if rev:
        cw = cw[::-1]
    m["conv_wb"] = np.ascontiguousarray(np.concatenate([cw, W["conv_b"]], axis=0))
    m["w_mod"] = W["w_mod"][0]
    m["b_mod"] = W["b_mod"]
    m["g_norm1"] = W["g_norm1"]
    m["g_norm2"] = W["g_norm2"]
    m["g_final"] = W["g_final"][None, :]
    m["w_in"] = W["w_in"][0]
    m["g_qnorm"] = W["g_qnorm"]
    m["g_knorm"] = W["g_knorm"]
    m["g_subln"] = W["g_subln"]
    m["lam_in"] = np.ascontiguousarray(np.concatenate([W["lambda_q1"], W["lambda_k1"], W["lambda_q2"], W["lambda_k2"]], axis=0))
    m["rel_bias"] = W["rel_bias"]
    m["w_out"] = W["w_out"][0]
    m["w_ffn"] = W["w_ffn_in"][0]
    m["w_down"] = W["w_down"][0]
    return {k: np.ascontiguousarray(np.asarray(v, dtype=np.float32)) for k, v in m.items()}


def _own_rows(rev, n_s, own_lo, own_hi):
    if not rev:
        assert own_lo == 0
    else:
        assert own_hi == n_s


_PROG_CACHE = {}


def _get_prog(key):
    if key not in _PROG_CACHE:
        _PROG_CACHE[key] = build_program(*key)
    return _PROG_CACHE[key]


def kernel(**inputs):
    W = {k: np.asarray(v, dtype=np.float32) for k, v in inputs.items()}
    x_prompt, x_sample = W["x_prompt"], W["x_sample"]
    c_prompt, c_sample = W["c_prompt"], W["c_sample"]
    n_cores = 8
    B, n_p, _ = x_prompt.shape
    Bs, n_s, _ = x_sample.shape
    NP = B // n_cores
    n_own = n_s // 2
    nc = _get_prog((NP, n_p, n_s, n_own, 2048))
    in_maps = []
    for i in range(n_cores):
        rev = (i % 2 == 1)
        si = i // 2
        xp_list = [x_prompt[NP * i + j] for j in range(NP)]
        c_list = [c_prompt[NP * i + j] for j in range(NP)] + [c_sample[si]]
        lo, hi = (0, n_own) if not rev else (n_s - n_own, n_s)
        in_maps.append(_core_inputs(rev, xp_list, x_sample[si], lo, hi, c_list, W))
    res = run_bass_kernel_spmd(nc, in_maps, core_ids=list(range(n_cores)))
    y_prompt = np.empty((B, n_p, D), np.float32)
    y_sample = np.empty((Bs, n_s, D), np.float32)
    for i in range(n_cores):
        rev = (i % 2 == 1)
        si = i // 2
        r = res.results[i]
        yp_ = np.asarray(r["yp"], dtype=np.float32)
        ys_ = np.asarray(r["ys"], dtype=np.float32)
        if rev:
            yp_ = yp_[:, ::-1]
            y_sample[si, n_s - n_own:] = ys_[::-1]
        else:
            y_sample[si, :n_own] = ys_
        y_prompt[NP * i:NP * (i + 1)] = yp_
    return (y_prompt, y_sample)
```
